# Optimizing a Trainium2 kernel written in Bass

```python
import jax, jax.numpy as jnp
from jax import lax
import numpy as np

D_MODEL = 2048
BATCH = 4
SEQ = 4096
DEPTH = 2

A_WIDTH = D_MODEL // 2
A_HEADS = 8
A_HEAD_DIM = A_WIDTH // A_HEADS
GMLP_CHUNK = 128
B_WIDTH = D_MODEL // 2
B_HEAD_DIM = 128
B_HEADS = B_WIDTH // B_HEAD_DIM
HGRN_CHUNK = 64
C_HEAD_DIM = 128
C_HEADS = D_MODEL // C_HEAD_DIM
ATTN_BLOCK = 128
FFN_HIDDEN = ((8 * D_MODEL // 3 + 255) // 256) * 256
CONV_WIDTH = 3
RMS_EPS = 1e-6
N_AB = (DEPTH + 1) // 2
N_C = DEPTH // 2
AB_IN = 2 * A_WIDTH + 4 * B_WIDTH
C_IN = 4 * D_MODEL + C_HEADS

kernel_name = "hybrid_gmlp_hgrn2_fox_convffn"


def rms_norm(x, gain):
    xf = x.astype(jnp.float32)
    y = xf * lax.rsqrt(jnp.mean(xf * xf, axis=-1, keepdims=True) + RMS_EPS)
    return (y * gain.astype(jnp.float32)).astype(x.dtype)


def hgrn2(q, f_logit, i, g, lower_bound, o_gain):
    bsz, seq, _ = q.shape
    dt = q.dtype
    f32 = jnp.float32
    shp = (bsz, seq, B_HEADS, B_HEAD_DIM)
    fl = f_logit.astype(f32).reshape(shp)
    lb = lower_bound.astype(f32).reshape(B_HEADS, B_HEAD_DIM)
    log_f = jnp.logaddexp(jnp.log(lb), jnp.log1p(-lb) + jax.nn.log_sigmoid(fl))
    k = (1.0 - lb) * jax.nn.sigmoid(-fl)
    nc = seq // HGRN_CHUNK

    def chunks(t):
        return t.reshape(bsz, nc, HGRN_CHUNK, B_HEADS, -1).transpose(1, 0, 3, 2, 4)

    qc = chunks(q.astype(f32).reshape(shp))
    kc = chunks(k)
    vc = chunks(i.astype(f32).reshape(shp))
    lfc = chunks(log_f)
    mask = jnp.tril(jnp.ones((HGRN_CHUNK, HGRN_CHUNK), dtype=bool))

    def step(state, xs):
        q_c, k_c, v_c, lf_c = xs
        G = jnp.cumsum(lf_c, axis=2)
        o_inter = jnp.einsum('bhtk,bhkv->bhtv', q_c * jnp.exp(G), state)
        diff = G[:, :, :, None, :] - G[:, :, None, :, :]
        decay = jnp.exp(jnp.where(mask[:, :, None], diff, -jnp.inf))
        scores = jnp.einsum('bhtk,bhtsk,bhsk->bhts', q_c, decay, k_c)
        o_intra = jnp.einsum('bhts,bhsv->bhtv', scores, v_c)
        G_end = G[:, :, -1:, :]
        new_state = (jnp.exp(G_end[:, :, 0, :])[..., None] * state
                     + jnp.einsum('bhsk,bhsv->bhkv', k_c * jnp.exp(G_end - G), v_c))
        return new_state, o_inter + o_intra

    s0 = jnp.zeros((bsz, B_HEADS, B_HEAD_DIM, B_HEAD_DIM), f32)
    _, o = lax.scan(step, s0, (qc, kc, vc, lfc))
    o = o.transpose(1, 0, 3, 2, 4).reshape(shp)
    o = rms_norm(o, o_gain) * jax.nn.silu(g.astype(f32).reshape(shp))
    return o.reshape(bsz, seq, B_WIDTH).astype(dt)


def mixer_ab(h, w_in, sp_w, sp_b, v_gain, lower_bound, o_gain, w_out):
    bsz, seq, _ = h.shape
    cuts = [A_WIDTH, 2 * A_WIDTH, 2 * A_WIDTH + B_WIDTH,
            2 * A_WIDTH + 2 * B_WIDTH, 2 * A_WIDTH + 3 * B_WIDTH]
    u, v, q, f, i, g = jnp.split(h @ w_in, cuts, axis=-1)
    u = jax.nn.gelu(u, approximate=False)
    v = jax.nn.gelu(v, approximate=False)
    v = rms_norm(v.reshape(bsz, seq, A_HEADS, A_HEAD_DIM), v_gain.reshape(A_HEADS, A_HEAD_DIM))
    v = v.reshape(bsz, seq // GMLP_CHUNK, GMLP_CHUNK, A_HEADS, A_HEAD_DIM)
    w_causal = sp_w * jnp.tril(jnp.ones((GMLP_CHUNK, GMLP_CHUNK), sp_w.dtype))
    mixed = jnp.einsum('hts,bcshd->bcthd', w_causal, v) + sp_b.T[:, :, None]
    y_a = u * mixed.reshape(bsz, seq, A_WIDTH)
    y_b = hgrn2(q, f, i, g, lower_bound, o_gain)
    return jnp.concatenate([y_a, y_b], axis=-1) @ w_out


def mixer_c(h, w_in, b_f, q_gain, k_gain, w_out):
    bsz, seq, _ = h.shape
    shp = (bsz, seq, C_HEADS, C_HEAD_DIM)
    q, k, v, g, f = jnp.split(h @ w_in, [D_MODEL, 2 * D_MODEL, 3 * D_MODEL, 4 * D_MODEL], axis=-1)
    q = rms_norm(q.reshape(shp), q_gain)
    k = rms_norm(k.reshape(shp), k_gain)
    v = v.reshape(shp)
    log_f = jax.nn.log_sigmoid(f.astype(jnp.float32) + b_f.astype(jnp.float32))
    c = jnp.cumsum(log_f, axis=1).transpose(0, 2, 1)
    scale = C_HEAD_DIM ** -0.5
    outs = []
    for blk in range(seq // ATTN_BLOCK):
        lo, hi = blk * ATTN_BLOCK, (blk + 1) * ATTN_BLOCK
        s = jnp.einsum('bthd,bshd->bhts', q[:, lo:hi], k[:, :hi]).astype(jnp.float32) * scale
        s = s + c[:, :, lo:hi, None] - c[:, :, None, :hi]
        causal = (lo + jnp.arange(ATTN_BLOCK))[:, None] >= jnp.arange(hi)[None, :]
        p = jax.nn.softmax(jnp.where(causal, s, -jnp.inf), axis=-1).astype(v.dtype)
        outs.append(jnp.einsum('bhts,bshd->bthd', p, v[:, :hi]))
    o = jnp.concatenate(outs, axis=1).reshape(bsz, seq, D_MODEL) * jax.nn.sigmoid(g)
    return o @ w_out


def conv_ffn(h, w_up, conv_w, conv_b, w_down):
    seq = h.shape[1]
    z = h @ w_up
    zp = jnp.pad(z, ((0, 0), (CONV_WIDTH - 1, 0), (0, 0)))
    z = sum(conv_w[j] * zp[:, j:j + seq] for j in range(CONV_WIDTH)) + conv_b
    a, b = jnp.split(z, 2, axis=-1)
    return (jax.nn.silu(a) * b) @ w_down


def setup_inputs(seed: int = 0) -> dict:
    key = jax.random.key(seed)
    ks = jax.random.split(key, 20)
    f32 = jnp.float32

    def dense(k, shape, fan_in):
        return jax.random.normal(k, shape, f32) * (fan_in ** -0.5)

    def gain(k, shape):
        return 1.0 + 0.02 * jax.random.normal(k, shape, f32)

    def small(k, shape, s=0.02):
        return s * jax.random.normal(k, shape, f32)

    return {
        "x": jax.random.normal(ks[0], (BATCH, SEQ, D_MODEL), f32),
        "mix_norm": gain(ks[1], (DEPTH, D_MODEL)),
        "ab_w_in": dense(ks[2], (N_AB, D_MODEL, AB_IN), D_MODEL),
        "ab_sp_w": dense(ks[3], (N_AB, A_HEADS, GMLP_CHUNK, GMLP_CHUNK), GMLP_CHUNK),
        "ab_sp_b": small(ks[4], (N_AB, A_HEADS, GMLP_CHUNK)),
        "ab_v_norm": gain(ks[5], (N_AB, A_WIDTH)),
        "hgrn_gamma": small(ks[6], (DEPTH + 1, B_WIDTH)),
        "hgrn_o_norm": gain(ks[7], (N_AB, B_HEAD_DIM)),
        "ab_w_out": dense(ks[8], (N_AB, A_WIDTH + B_WIDTH, D_MODEL), A_WIDTH + B_WIDTH),
        "c_w_in": dense(ks[9], (N_C, D_MODEL, C_IN), D_MODEL),
        "c_b_f": small(ks[10], (N_C, C_HEADS), 0.1),
        "c_q_norm": gain(ks[11], (N_C, C_HEAD_DIM)),
        "c_k_norm": gain(ks[12], (N_C, C_HEAD_DIM)),
        "c_w_out": dense(ks[13], (N_C, D_MODEL, D_MODEL), D_MODEL),
        "ffn_norm": gain(ks[14], (DEPTH, D_MODEL)),
        "ffn_w_up": dense(ks[15], (DEPTH, D_MODEL, 2 * FFN_HIDDEN), D_MODEL),
        "ffn_conv_w": dense(ks[16], (DEPTH, CONV_WIDTH, 2 * FFN_HIDDEN), CONV_WIDTH),
        "ffn_conv_b": small(ks[17], (DEPTH, 2 * FFN_HIDDEN)),
        "ffn_w_down": dense(ks[18], (DEPTH, FFN_HIDDEN, D_MODEL), FFN_HIDDEN),
    }


def reference(x, mix_norm, ab_w_in, ab_sp_w, ab_sp_b, ab_v_norm, hgrn_gamma, hgrn_o_norm,
              ab_w_out, c_w_in, c_b_f, c_q_norm, c_k_norm, c_w_out, ffn_norm, ffn_w_up,
              ffn_conv_w, ffn_conv_b, ffn_w_down):
    lb_table = jnp.cumsum(jax.nn.softmax(hgrn_gamma.astype(jnp.float32), axis=0), axis=0)
    for l in range(DEPTH):
        j = l // 2
        h = rms_norm(x, mix_norm[l])
        if l % 2 == 0:
            h = mixer_ab(h, ab_w_in[j], ab_sp_w[j], ab_sp_b[j], ab_v_norm[j],
                         lb_table[l], hgrn_o_norm[j], ab_w_out[j])
        else:
            h = mixer_c(h, c_w_in[j], c_b_f[j], c_q_norm[j], c_k_norm[j], c_w_out[j])
        x = x + h
        h = rms_norm(x, ffn_norm[l])
        x = x + conv_ffn(h, ffn_w_up[l], ffn_conv_w[l], ffn_conv_b[l], ffn_w_down[l])
    return x
```

```python
import numpy as np
import concourse.bass as bass
import concourse.mybir as mybir
from concourse.bass_utils import run_bass_kernel_spmd
from contextlib import ExitStack

F32 = mybir.dt.float32
BF16 = mybir.dt.bfloat16
AF = mybir.ActivationFunctionType
ALU = mybir.AluOpType
AX = mybir.AxisListType

ENGS = ("pe", "act", "dve", "pool", "sp")
EPS = 1e-6
D = 2048
FH = 5632
GT = 512


class Buf:
    __slots__ = ("name", "last_write", "reads", "sem", "dcount")

    def __init__(self, name):
        self.name = name
        self.last_write = None
        self.reads = {}
        self.sem = None
        self.dcount = 0


def _tkey(t):
    return t[1] if t[0] == "c" else id(t[1])


class _Op:
    __slots__ = ("fn", "deps", "need_inc", "dma_buf", "val")

    def __init__(self, fn, deps):
        self.fn = fn
        self.deps = deps
        self.need_inc = False
        self.dma_buf = None
        self.val = 0


class Sched:
    def __init__(self, nc):
        self.nc = nc
        self.ops = {e: [] for e in ENGS}
        self.dma_bufs = []
        self.same_eng_window = 2

    def _deps(self, eng, reads, writes, tok):
        deps = []
        raw = set()
        for b in reads:
            if b.last_write is not None:
                deps.append(b.last_write)
                raw.add(b.last_write)
        for b in writes:
            if b.last_write is not None:
                deps.append(b.last_write)
            deps.extend(b.reads.values())
        for b in reads:
            b.reads[_tkey(tok)] = tok
        for b in writes:
            b.last_write = tok
            b.reads = {}
        out = []
        seen = set()
        for d in deps:
            if d[0] == "c" and d[1] == eng:
                if not (d in raw and tok[0] == "c" and tok[2] - d[2] <= self.same_eng_window):
                    continue
            key = (d[0], d[1] if d[0] == "c" else id(d[1]), d[2])
            if key in seen:
                continue
            seen.add(key)
            out.append(d)
        return out

    def alias(self, old, new):
        toks = {}
        for b in old:
            cands = list(b.reads.values())
            if b.last_write is not None:
                cands.append(b.last_write)
            for t in cands:
                k = _tkey(t)
                if k not in toks or toks[k][2] < t[2]:
                    toks[k] = t
        for b in new:
            b.last_write = None
            b.reads = dict(toks)

    def op(self, eng, fn, reads=(), writes=()):
        idx = len(self.ops[eng])
        tok = ("c", eng, idx)
        deps = self._deps(eng, reads, writes, tok)
        self.ops[eng].append(_Op(fn, deps))
        return tok

    def dma(self, queue, out, in_, reads=(), writes=(), sbuf=None, slow=False):
        if sbuf.sem is None:
            self.dma_bufs.append(sbuf)
            sbuf.sem = True
        sbuf.dcount += 16
        tok = ("d", sbuf, sbuf.dcount)
        deps = self._deps(queue, reads, writes, tok)
        if slow:
            fn = lambda e: e.dma_start(out=out, in_=in_, allow_slow_non_contiguous=True)
        else:
            fn = lambda e: e.dma_start(out=out, in_=in_)
        o = _Op(fn, deps)
        o.dma_buf = sbuf
        self.ops[queue].append(o)
        return tok

    def emit(self, final_waits=()):
        nc = self.nc
        for e in ENGS:
            for o in self.ops[e]:
                for d in o.deps:
                    if d[0] == "c":
                        self.ops[d[1]][d[2]].need_inc = True
        for e in ENGS:
            c = 0
            for o in self.ops[e]:
                if o.need_inc:
                    c += 1
                o.val = c
        with ExitStack() as st:
            esem = {e: st.enter_context(nc.semaphore("s_" + e)) for e in ENGS}
            for b in self.dma_bufs:
                b.sem = st.enter_context(nc.semaphore("d_" + b.name))
            block = st.enter_context(nc.Block())

            def run(ename, eng):
                known = {}
                for o in self.ops[ename]:
                    for d in o.deps:
                        if d[0] == "c":
                            sem = esem[d[1]]
                            v = self.ops[d[1]][d[2]].val
                            key = d[1]
                        else:
                            sem = d[1].sem
                            v = d[2]
                            key = id(d[1])
                        if known.get(key, 0) >= v:
                            continue
                        known[key] = v
                        eng.wait_ge(sem, v)
                    ins = o.fn(eng)
                    if o.dma_buf is not None:
                        ins.then_inc(o.dma_buf.sem, 16)
                    elif o.need_inc:
                        ins.then_inc(esem[ename], 1)
                if ename == "sp":
                    for b in final_waits:
                        eng.wait_ge(b.sem, b.dcount)

            @block.tensor
            def _(eng):
                run("pe", eng)

            @block.scalar
            def _(eng):
                run("act", eng)

            @block.vector
            def _(eng):
                run("dve", eng)

            @block.gpsimd
            def _(eng):
                run("pool", eng)

            @block.sync
            def _(eng):
                run("sp", eng)


WSHAPES = {
    "ab_in": (2048, 6144), "ab_out": (2048, 2048), "up0": (2048, 11264), "dn0": (5632, 2048),
    "c_in": (2048, 8208), "c_out": (2048, 2048), "up1": (2048, 11264), "dn1": (5632, 2048),
}
AB_IN_COLS = [0, 512, 1024, 1536, 3072, 2048, 3584, 2560, 4096, 4608, 5120, 5632]


def weight_seq_group():
    seq = []
    for c0 in AB_IN_COLS:
        seq.append(("ab_in", 0, 16, c0, 512))
    for cb in range(4):
        seq.append(("ab_out", 0, 16, cb * 512, 512))

    def ffn(l):
        for bi in range(22):
            seq.append(("up%d" % l, 0, 16, (bi * 256, FH + bi * 256), 256))
        for cb in range(4):
            for kb in range(3):
                seq.append(("dn%d" % l, kb * 2048, 16 if kb < 2 else 12, cb * 512, 512))
    ffn(0)
    for cb in range(16):
        seq.append(("c_in", 0, 16, cb * 512, 512))
    for cb in range(4):
        seq.append(("c_out", 0, 16, cb * 512, 512))
    ffn(1)
    return seq


def build(NG=8, stop=None, dbg=None):
    T = NG * GT
    nc = bass.Bass("TRN2", target_bir_lowering=False)

    def din(name, shape):
        return nc.dram_tensor(name, shape, F32, kind="ExternalInput").ap()

    x = din("x", [T, D])
    mix_norm = din("mix_norm", [32, 128])
    ffn_norm = din("ffn_norm", [32, 128])
    conv_w = din("ffn_conv_w", [528, 128])
    conv_b = din("ffn_conv_b", [176, 128])
    gamma = din("hgrn_gamma", [24, 128])
    vecs = din("vecs", [3, 128])
    v_norm = din("ab_v_norm", [1024])
    sp_w = din("ab_sp_w", [8, 128, 128])
    sp_b = din("ab_sp_b", [1, 1024])
    b_f = din("c_b_f", [16, 1])
    wf32 = {
        "ab_in": din("ab_w_in", [2048, 6144]), "ab_out": din("ab_w_out", [2048, 2048]),
        "up0": din("ffn_w_up0", [2048, 11264]), "up1": din("ffn_w_up1", [2048, 11264]),
        "dn0": din("ffn_w_down0", [5632, 2048]), "dn1": din("ffn_w_down1", [5632, 2048]),
        "c_in": din("c_w_in", [2048, 8208]), "c_out": din("c_w_out", [2048, 2048]),
    }
    out = nc.dram_tensor("out", [T, D], F32, kind="ExternalOutput").ap()
    dbg_y = nc.dram_tensor("dbg_y", [128, 16, GT], BF16, kind="ExternalOutput").ap() if dbg else None
    wbf = {k: nc.dram_tensor("wb_" + k, list(WSHAPES[k]), BF16).ap() for k in WSHAPES}
    kT_d = nc.dram_tensor("kT_d", [16, 128, T], BF16).ap()
    v_d = nc.dram_tensor("v_d", [16, 128, T // 128, 128], BF16).ap()

    S = Sched(nc)
    with ExitStack() as st:
        def sb(name, shape, dt):
            return st.enter_context(nc.sbuf_tensor(name, shape, dt))

        ident_f = sb("ident_f", [128, 128], F32)
        ident_b = sb("ident_b", [128, 128], BF16)
        ones_b = sb("ones_b", [128, 128], BF16)
        ones_f = sb("ones_f", [128, 512], F32)
        maskneg = sb("maskneg", [128, 128], F32)
        hmask = sb("hmask", [64, 8, 64], BF16)
        rmask = sb("rmask", [128, 512], F32)
        zhalo = [sb("zhalo%d" % l, [128, 88, 2], F32) for l in range(2)]
        S32 = sb("S32", [128, 8, 128], F32)
        ccarry = sb("ccarry", [16, 1], F32)
        negcT = sb("negcT", [128, (T // 128) * 16], F32)
        gmix = sb("gmix", [128, 32], F32)
        gffn = sb("gffn", [128, 32], F32)
        cw = sb("cw", [128, 528], F32)
        cbias = sb("cbias", [128, 176], F32)
        gam = sb("gam", [128, 24], F32)
        lbt = sb("lbt", [128, 8], F32)
        oml = sb("oml", [128, 8], F32)
        vec3 = sb("vec3", [128, 3], F32)
        vgain = sb("vgain", [128, 1024], F32)
        WcT = sb("WcT", [128, 8, 128], BF16)
        spb = sb("spb", [1, 1024], F32)
        negbf = sb("negbf", [16, 1], F32)
        wf = sb("wf", [128, 16, 16], BF16)
        stg = sb("stg", [128, 128], F32)
        eGend = sb("eGend", [128, 64], F32)
        xin = [sb("xin0", [128, D], F32)] * 2
        xT = sb("xT", [128, 16, GT], F32)
        hT = sb("hT", [128, 16, GT], BF16)
        NWB = 2
        wt = [sb("wt%d" % i, [128, 16, 512], BF16) for i in range(NWB)]
        NSQ = 4
        sqs = [sb("sq%d" % i, [128, GT], BF16) for i in range(NSQ)]
        NTMP = 6
        tmps = [sb("tmp%d" % i, [128, 514], F32) for i in range(NTMP)]
        yT = sb("yT", [128, 16, GT], BF16)
        Sst = [sb("Sst%d" % i, [128, 8, 128], BF16) for i in range(2)]
        sTt = [sb("sT%d" % i, [64, GT], BF16) for i in range(2)]
        M3 = sb("M3", [128, 22528], BF16)
        actT = M3[:, :].rearrange("p (j t) -> p j t", t=GT)
        vn = M3[:, 0:4096].rearrange("p (n f) -> p n f", n=4)
        kT = M3[:, 4096:8192].rearrange("p (h t) -> p h t", h=8)
        eG = M3[:, 8192:12288].bitcast(F32).rearrange("p (m t) -> p m t", m=4)
        it = M3[0:64, 12288:20480].rearrange("p (c f) -> p c f", c=8)
        ktok = [M3[0:64, 20480 + i * 1024:20480 + (i + 1) * 1024].rearrange("p (c k) -> p c k", c=8) for i in range(2)]
        vst = [M3[:, i * 2048:(i + 1) * 2048].rearrange("p (n f) -> p n f", n=4) for i in range(2)]
        NKV = 3
        kpc = [M3[:, 4096 + i * 1024:4096 + (i + 1) * 1024] for i in range(NKV)]
        vpc = [M3[:, 7168 + i * 1024:7168 + (i + 1) * 1024].rearrange("p (j d) -> p j d", j=8) for i in range(NKV)]
        cbt = [M3[:, 10240 + i * 1024:10240 + (i + 1) * 1024].bitcast(F32) for i in range(2)]
        kst = [M3[:, 12288 + i * 512:12288 + (i + 1) * 512] for i in range(2)]
        pTt = [M3[:, 13312 + i * 512:13312 + (i + 1) * 512] for i in range(3)]
        c16 = M3[0:16, 14848:15872].bitcast(F32)
        e16 = M3[0:16, 15872:16896].bitcast(F32)
        psb = [st.enter_context(nc.psum_tensor("ps%d" % i, [128, 512], F32)) for i in range(8)]

        Bc = Buf("consts")
        PB = [Buf("ps%d" % i) for i in range(8)]
        xinb = [Buf("xin0")] * 2
        xTb = [Buf("xT%d" % c) for c in range(16)]
        hTb = [Buf("hT%d" % c) for c in range(16)]
        wtb = [Buf("wt%d" % i) for i in range(NWB)]
        sqb = [Buf("sq%d" % i) for i in range(NSQ)]
        tmpb = [Buf("tmp%d" % i) for i in range(NTMP)]
        yTb = [Buf("yT%d" % c) for c in range(16)]
        vnb = [Buf("vn%d" % n) for n in range(4)]
        kTb = [Buf("kT%d" % h) for h in range(8)]
        eGb = [Buf("eG%d" % m) for m in range(4)]
        itb = Buf("it")
        ktokb = [Buf("ktok%d" % i) for i in range(2)]
        Sstb = [Buf("Sst%d" % i) for i in range(2)]
        sTb = [Buf("sT%d" % i) for i in range(2)]
        actb = [Buf("act%d" % j) for j in range(44)]
        kstb = [Buf("kst%d" % i) for i in range(2)]
        vstb = [Buf("vst%d" % i) for i in range(2)]
        kpcb = [Buf("kpc%d" % i) for i in range(NKV)]
        vpcb = [Buf("vpc%d" % i) for i in range(NKV)]
        cbtb = [Buf("cbt%d" % i) for i in range(2)]
        pTb = [Buf("pT%d" % i) for i in range(3)]
        c16b = Buf("c16")
        e16b = Buf("e16")
        S32b = [Buf("S32_%d" % h) for h in range(8)]
        eGendb = Buf("eGend")
        zhb = [Buf("zh%d" % l) for l in range(2)]
        ccb = Buf("ccarry")
        negcb = Buf("negcT")
        stgb = Buf("stg")
        wfb = Buf("wf")
        parb = Buf("params")
        vgb = Buf("vgain")
        spbb = Buf("spb")
        bfb = Buf("bf")
        wmatb = {k: Buf("wm_" + k) for k in WSHAPES}
        kdb = [Buf("kd%d" % h) for h in range(16)]
        vdb = [Buf("vd%d" % h) for h in range(16)]

        dumps = {}

        def dump(name, ap, shape, dt, bufs):
            if dbg != "hg" or name in dumps:
                return
            dumps[name] = nc.dram_tensor("dbg_" + name, shape, dt, kind="ExternalOutput").ap()
            S.dma("sp", dumps[name], ap, reads=bufs, writes=[], sbuf=bufs[0])

        cnt = {"mm": 0, "aux": 0, "sq": 0, "tmp": 0}

        def psum(pool):
            if pool == "mm":
                i = cnt["mm"] % 4
                cnt["mm"] += 1
            else:
                i = 4 + cnt["aux"] % 4
                cnt["aux"] += 1
            return psb[i], PB[i]

        def sqbuf():
            i = cnt["sq"] % NSQ
            cnt["sq"] += 1
            return sqs[i], sqb[i]

        def tmp():
            i = cnt["tmp"] % NTMP
            cnt["tmp"] += 1
            return tmps[i], tmpb[i]

        def mm(o, lhsT, rhs, start, stop, reads, pb):
            S.op("pe", lambda e: e.matmul(o, lhsT=lhsT, rhs=rhs, start=start, stop=stop), reads=reads, writes=[pb])

        def tr(o, in_, ident, reads, pb):
            S.op("pe", lambda e: e.transpose(out=o, in_=in_, identity=ident), reads=reads, writes=[pb])

        def act(o, in_, func, reads, writes, **kw):
            S.op("act", lambda e: e.activation(out=o, in_=in_, func=func, **kw), reads=reads, writes=writes)

        def stt(o, in0, scalar, in1, op0, op1, reads, writes, eng="dve"):
            S.op(eng, lambda e: e.scalar_tensor_tensor(out=o, in0=in0, scalar=scalar, in1=in1, op0=op0, op1=op1),
                 reads=reads, writes=writes)

        def tt(o, in0, in1, op, reads, writes, eng="dve"):
            S.op(eng, lambda e: e.tensor_tensor(out=o, in0=in0, in1=in1, op=op), reads=reads, writes=writes)

        def ts(o, in0, s1, op0, reads, writes, s2=None, op1=None, eng="dve"):
            if op1 is None:
                S.op(eng, lambda e: e.tensor_scalar(out=o, in0=in0, scalar1=s1, scalar2=None, op0=op0),
                     reads=reads, writes=writes)
            else:
                S.op(eng, lambda e: e.tensor_scalar(out=o, in0=in0, scalar1=s1, scalar2=s2, op0=op0, op1=op1),
                     reads=reads, writes=writes)

        def cp(o, in_, reads, writes, eng="dve"):
            if eng == "act":
                S.op(eng, lambda e: e.activation(out=o, in_=in_, func=AF.Copy), reads=reads, writes=writes)
            else:
                S.op(eng, lambda e: e.tensor_copy(out=o, in_=in_), reads=reads, writes=writes)

        def recip(o, in_, reads, writes):
            S.op("dve", lambda e: e.reciprocal(out=o, in_=in_), reads=reads, writes=writes)

        def memset(ap, v, writes, eng="pool"):
            S.op(eng, lambda e: e.memset(ap, v), writes=writes)

        def asel(o, in_, pattern, cmp, fill, base, cm, reads, writes):
            S.op("pool", lambda e: e.affine_select(out=o, in_=in_, pattern=pattern, compare_op=cmp, fill=fill,
                                                   base=base, channel_multiplier=cm), reads=reads, writes=writes)

        memset(ident_f[:], 0.0, [Bc])
        asel(ident_f[:], ident_f[:], [[-1, 128]], ALU.not_equal, 1.0, 0, 1, [Bc], [Bc])
        cp(ident_b[:], ident_f[:], [Bc], [Bc], eng="pool")
        memset(ones_b[:], 1.0, [Bc])
        memset(ones_f[:], 1.0, [Bc])
        memset(maskneg[:], 0.0, [Bc])
        asel(maskneg[:], maskneg[:], [[1, 128]], ALU.is_ge, -30000.0, 0, -1, [Bc], [Bc])
        memset(hmask[:], 1.0, [Bc])
        asel(hmask[:], hmask[:], [[0, 8], [1, 64]], ALU.is_ge, 0.0, 0, -1, [Bc], [Bc])
        memset(rmask[:], 1.0, [Bc])
        memset(rmask[:].rearrange("p (c t) -> p c t", t=64)[:, :, 0:1], 0.0, [Bc])
        for l in range(2):
            memset(zhalo[l][:], 0.0, [zhb[l]])
        memset(S32[:], 0.0, S32b)
        memset(ccarry[:], 0.0, [ccb])

        for k in ["ab_in", "ab_out", "up0", "dn0", "c_in", "c_out", "up1", "dn1"]:
            rows = WSHAPES[k][0]
            step = 256
            for r0 in range(0, rows, step):
                S.dma("pool", wbf[k][r0:r0 + step, :], wf32[k][r0:r0 + step, :], writes=[wmatb[k]], sbuf=wmatb[k])

        def load_fm(src2d, nrows, dst_ap):
            S.dma("sp", stg[0:nrows, :], src2d, writes=[stgb], sbuf=stgb)
            ps, pb = psum("mm")
            tr(ps[:, 0:nrows], stg[0:nrows, :], ident_f[0:nrows, 0:nrows], [stgb, Bc], pb)
            cp(dst_ap, ps[:, 0:nrows], [pb], [parb])

        load_fm(mix_norm, 32, gmix[:, :])
        load_fm(ffn_norm, 32, gffn[:, :])
        for i in range(5):
            r0 = i * 128
            n = min(128, 528 - r0)
            load_fm(conv_w[r0:r0 + n, :], n, cw[:, r0:r0 + n])
        for i in range(2):
            r0 = i * 128
            n = min(128, 176 - r0)
            load_fm(conv_b[r0:r0 + n, :], n, cbias[:, r0:r0 + n])
        load_fm(gamma, 24, gam[:, :])
        load_fm(vecs, 3, vec3[:, :])
        ogain = vec3[:, 0:1]
        qgain = vec3[:, 1:2]
        kgain = vec3[:, 2:3]
        act(gam[:, :], gam[:, :], AF.Exp, [parb], [parb])
        tt(lbt[:, :], gam[:, 0:8], gam[:, 8:16], ALU.add, [parb], [parb])
        tt(lbt[:, :], lbt[:, :], gam[:, 16:24], ALU.add, [parb], [parb])
        recip(lbt[:, :], lbt[:, :], [parb], [parb])
        tt(lbt[:, :], lbt[:, :], gam[:, 0:8], ALU.mult, [parb], [parb])
        ts(oml[:, :], lbt[:, :], -1.0, ALU.mult, [parb], [parb], s2=1.0, op1=ALU.add)
        for h in range(8):
            S.dma("sp", stg[:, :], sp_w[h], writes=[stgb], sbuf=stgb)
            ps, pb = psum("mm")
            tr(ps[:, 0:128], stg[:, :], ident_f[:, :], [stgb, Bc], pb)
            t_, tb_ = tmp()
            cp(t_[:, 0:128], ps[:, 0:128], [pb], [tb_])
            asel(t_[:, 0:128], t_[:, 0:128], [[1, 128]], ALU.is_ge, 0.0, 0, -1, [tb_], [tb_])
            cp(WcT[:, h, :], t_[:, 0:128], [tb_], [parb], eng="pool")
        S.dma("sp", vgain[:, :], v_norm.partition_broadcast(128), writes=[vgb], sbuf=vgb)
        S.dma("sp", spb[:, :], sp_b, writes=[spbb], sbuf=spbb)
        S.dma("sp", negbf[:, :], b_f, writes=[bfb], sbuf=bfb)
        ts(negbf[:, :], negbf[:, :], -1.0, ALU.mult, [bfb], [bfb])
        S.dma("sp", wf[:, :, :], wbf["c_in"][:, 8192:8208].rearrange("(c p) n -> p c n", p=128),
              reads=[wmatb["c_in"]], writes=[wfb], sbuf=wfb, slow=True)

        gseq = weight_seq_group()
        wseq = gseq * NG
        wstate = {"pos": 0, "loaded": 0}

        def wload(i):
            name, r0, nk, c0, ncol = wseq[i]
            slot = i % NWB
            c0s = c0 if isinstance(c0, tuple) else (c0,)
            for ci, cc in enumerate(c0s):
                src = wbf[name][r0:r0 + nk * 128, cc:cc + ncol].rearrange("(c p) n -> p c n", p=128)
                S.dma("sp", wt[slot][:, 0:nk, ci * ncol:(ci + 1) * ncol], src, reads=[wmatb[name]],
                      writes=[wtb[slot]], sbuf=wtb[slot])

        def wacq(spec):
            i = wstate["pos"]
            assert wseq[i] == spec, (i, wseq[i], spec)
            lim = min(len(wseq), i + NWB)
            while wstate["loaded"] < lim:
                wload(wstate["loaded"])
                wstate["loaded"] += 1
            wstate["pos"] += 1
            s = i % NWB
            return wt[s], wtb[s]

        def load_group(g):
            t0 = g * GT
            for n in range(4):
                xi, xib = xin[n % 2], xinb[n % 2]
                S.dma("sp", xi[:, :], x[t0 + n * 128:t0 + (n + 1) * 128, :], writes=[xib], sbuf=xib)
                for q4 in range(4):
                    ps, pb = psum("mm")
                    for j in range(4):
                        c = 4 * q4 + j
                        tr(ps[:, j * 128:(j + 1) * 128], xi[:, c * 128:(c + 1) * 128], ident_f[:, :], [xib, Bc], pb)
                    cp(xT[:, 4 * q4:4 * q4 + 4, n * 128:(n + 1) * 128],
                       ps[:, :].rearrange("p (j t) -> p j t", j=4), [pb], xTb[4 * q4:4 * q4 + 4],
                       eng=("dve" if q4 % 2 == 0 else "act") if False else "dve")

        def store_group(g):
            t0 = g * GT
            for n in range(4):
                xi, xib = xin[n % 2], xinb[n % 2]
                for q4 in range(4):
                    ps, pb = psum("mm")
                    for j in range(4):
                        c = 4 * q4 + j
                        tr(ps[:, j * 128:(j + 1) * 128], xT[:, c, n * 128:(n + 1) * 128], ident_f[:, :], [xTb[c], Bc], pb)
                    S.op("act", lambda e, o=xi[:, q4 * 512:(q4 + 1) * 512], i_=ps[:, :]: e.activation(out=o, in_=i_, func=AF.Copy),
                         reads=[pb], writes=[xib])
                S.dma("sp", out[t0 + n * 128:t0 + (n + 1) * 128, :], xi[:, :], reads=[xib], writes=[], sbuf=xib)

        def norm(gt_, col0):
            ps, pb = psum("mm")
            for c in range(16):
                sq, sqb_ = sqbuf()
                act(sq[:, :], xT[:, c, :], AF.Square, [xTb[c]], [sqb_])
                mm(ps[:, :], ones_b[:, :], sq[:, :], c == 0, c == 15, [sqb_, Bc], pb)
            rs, rsb = tmp()
            act(rs[:, 0:512], ps[:, :], AF.Sqrt, [pb], [rsb], scale=1.0 / D, bias=EPS)
            recip(rs[:, 0:512], rs[:, 0:512], [rsb], [rsb])
            for c in range(16):
                stt(hT[:, c, :], xT[:, c, :], gt_[:, col0 + c:col0 + c + 1], rs[:, 0:512], ALU.mult, ALU.mult,
                    [xTb[c], rsb, parb], [hTb[c]])

        def proj_fm(w, wb_, nk, m, rhs_t, rhs_b, ps, pb, k0=0, first=True, last=True):
            for kc in range(nk):
                mm(ps[:, :], w[:, kc, m * 128:(m + 1) * 128], rhs_t[:, k0 + kc, :],
                   first and kc == 0, last and kc == nk - 1, [wb_, rhs_b[k0 + kc]], pb)

        def mixer_ab(g):
            norm(gmix, 0)
            for b in range(2):
                w, wb_ = wacq(("ab_in", 0, 16, AB_IN_COLS[b], 512))
                for m in range(4):
                    ps, pb = psum("mm")
                    proj_fm(w, wb_, 16, m, hT, hTb, ps, pb)
                    act(yT[:, 4 * b + m, :], ps[:, :], AF.Gelu, [pb], [yTb[4 * b + m]])
            for b in range(2):
                w, wb_ = wacq(("ab_in", 0, 16, AB_IN_COLS[2 + b], 512))
                for n in range(4):
                    ps, pb = psum("mm")
                    for kc in range(16):
                        mm(ps[:, :], hT[:, kc, n * 128:(n + 1) * 128], w[:, kc, :], kc == 0, kc == 15, [wb_, hTb[kc]], pb)
                    vg, vgb_ = tmp()
                    act(vg[:, 0:512], ps[:, :], AF.Gelu, [pb], [vgb_])
                    s2, s2b = tmp()
                    tt(s2[:, 0:512], vg[:, 0:512], vg[:, 0:512], ALU.mult, [vgb_], [s2b])
                    r4, r4b = tmp()
                    S.op("dve", lambda e, o=r4[:, 0:4], i_=s2[:, 0:512].rearrange("p (h d) -> p h d", h=4):
                         e.tensor_reduce(out=o, in_=i_, axis=AX.X, op=ALU.add), reads=[s2b], writes=[r4b])
                    act(r4[:, 0:4], r4[:, 0:4], AF.Sqrt, [r4b], [r4b], scale=1.0 / 128, bias=EPS)
                    recip(r4[:, 0:4], r4[:, 0:4], [r4b], [r4b])
                    for hh in range(4):
                        h = 4 * b + hh
                        stt(vn[:, n, h * 128:(h + 1) * 128], vg[:, hh * 128:(hh + 1) * 128], r4[:, hh:hh + 1],
                            vgain[:, h * 128:(h + 1) * 128], ALU.mult, ALU.mult, [vgb_, r4b, vgb], [vnb[n]])
            for h in range(8):
                ps, pb = psum("mm")
                for n in range(4):
                    mm(ps[:, n * 128:(n + 1) * 128], vn[:, n, h * 128:(h + 1) * 128], WcT[:, h, :], True, False,
                       [vnb[n], parb], pb)
                    mm(ps[:, n * 128:(n + 1) * 128], ones_f[0:1, 0:128], spb[0:1, h * 128:(h + 1) * 128], False, True,
                       [Bc, spbb], pb)
                tt(yT[:, h, :], ps[:, :], yT[:, h, :], ALU.mult, [pb, yTb[h]], [yTb[h]])
            for b in range(2):
                w, wb_ = wacq(("ab_in", 0, 16, AB_IN_COLS[4 + 2 * b], 512))
                for m in range(4):
                    h = 4 * b + m
                    ps, pb = psum("mm")
                    proj_fm(w, wb_, 16, m, hT, hTb, ps, pb)
                    t1, t1b = tmp()
                    act(t1[:, 0:512], ps[:, :], AF.Sigmoid, [pb], [t1b], scale=-1.0)
                    ts(t1[:, 0:512], t1[:, 0:512], oml[:, h:h + 1], ALU.mult, [t1b, parb], [t1b])
                    t2, t2b = tmp()
                    act(t2[:, 0:512], t1[:, 0:512], AF.Ln, [t1b], [t2b], scale=-1.0, bias=1.0)
                    G, Gb = tmp()
                    S.op("dve", lambda e, o=G[:, 0:512], d0=rmask[:, :], d1=t2[:, 0:512]:
                         e.tensor_tensor_scan(out=o, data0=d0, data1=d1, initial=0.0, op0=ALU.mult, op1=ALU.add),
                         reads=[t2b, Bc], writes=[Gb])
                    t3, t3b = tmp()
                    act(t3[:, 0:512], G[:, 0:512], AF.Exp, [Gb], [t3b], scale=-1.0)
                    tt(kT[:, h, :], t1[:, 0:512], t3[:, 0:512], ALU.mult, [t1b, t3b], [kTb[h]])
                    act(eG[:, m, :], G[:, 0:512], AF.Exp, [Gb], [eGb[m]])
                    cp(eGend[:, h * 8:(h + 1) * 8].rearrange("p (c o) -> p c o", o=1),
                       eG[:, m, :].rearrange("p (c t) -> p c t", t=64)[:, :, 63:64], [eGb[m]], [eGendb], eng="act")
                w, wb_ = wacq(("ab_in", 0, 16, AB_IN_COLS[5 + 2 * b], 512))
                for m in range(4):
                    h = 4 * b + m
                    ps, pb = psum("mm")
                    proj_fm(w, wb_, 16, m, hT, hTb, ps, pb)
                    tt(yT[:, 8 + h, :], ps[:, :], eG[:, m, :], ALU.mult, [pb, eGb[m]], [yTb[8 + h]])
            for b in range(2):
                w, wb_ = wacq(("ab_in", 0, 16, AB_IN_COLS[8 + b], 512))
                for cch in range(8):
                    ps, pb = psum("mm")
                    for kc in range(16):
                        mm(ps[0:64, :], hT[:, kc, cch * 64:(cch + 1) * 64], w[:, kc, :], kc == 0, kc == 15,
                           [wb_, hTb[kc]], pb)
                    act(it[0:64, cch, b * 512:(b + 1) * 512], ps[0:64, :], AF.Copy, [pb], [itb])

            def pre(h):
                sl = h % 2
                pst, pstb = psum("mm")
                pbt = pst[:, :].bitcast(BF16)
                for cch in range(8):
                    tr(pbt[0:64, cch * 128:(cch + 1) * 128], kT[:, h, cch * 64:(cch + 1) * 64], ident_b[:, :],
                       [kTb[h], Bc], pstb)
                act(ktok[sl][0:64, :, :], pbt[0:64, 0:1024].rearrange("p (c k) -> p c k", c=8), AF.Copy, [pstb], [ktokb[sl]])
                pkv = [psum("aux"), psum("aux")]
                for cch in range(8):
                    pk, pkb = pkv[cch // 4]
                    mm(pk[:, (cch % 4) * 128:(cch % 4 + 1) * 128], ktok[sl][0:64, cch, :], it[0:64, cch, h * 128:(h + 1) * 128],
                       True, True, [ktokb[sl], itb], pkb)
                act(Sst[sl][:, 0, :], S32[:, h, :], AF.Copy, [S32b[h]], [Sstb[sl]])
                for cch in range(8):
                    pk, pkb = pkv[cch // 4]
                    tt(S32[:, h, :], S32[:, h, :], pk[:, (cch % 4) * 128:(cch % 4 + 1) * 128], ALU.add, [S32b[h], pkb], [S32b[h]])
                    ts(S32[:, h, :], S32[:, h, :], eGend[:, h * 8 + cch:h * 8 + cch + 1], ALU.mult, [S32b[h], eGendb], [S32b[h]])
                    if cch < 7:
                        act(Sst[sl][:, cch + 1, :], S32[:, h, :], AF.Copy, [S32b[h]], [Sstb[sl]])
                psc, pscb = psum("mm")
                for cch in range(8):
                    mm(psc[0:64, cch * 64:(cch + 1) * 64], kT[:, h, cch * 64:(cch + 1) * 64], yT[:, 8 + h, cch * 64:(cch + 1) * 64],
                       True, True, [kTb[h], yTb[8 + h]], pscb)
                tt(sTt[sl][0:64, :], psc[0:64, :], hmask[0:64, :, :].rearrange("p c t -> p (c t)"), ALU.mult,
                   [pscb, Bc], [sTb[sl]])

            def post(h):
                sl = h % 2
                po, pob = psum("aux")
                for cch in range(8):
                    cs = slice(cch * 64, (cch + 1) * 64)
                    mm(po[:, cs], it[0:64, cch, h * 128:(h + 1) * 128], sTt[sl][0:64, cs], True, False, [itb, sTb[sl]], pob)
                    mm(po[:, cs], Sst[sl][:, cch, :], yT[:, 8 + h, cs], False, True, [Sstb[sl], yTb[8 + h]], pob)
                sq, sqb_ = sqbuf()
                act(sq[:, :], po[:, :], AF.Square, [pob], [sqb_])
                pss, pssb = psum("mm")
                mm(pss[:, :], ones_b[:, :], sq[:, :], True, True, [sqb_, Bc], pssb)
                rs, rsb = tmp()
                act(rs[:, 0:512], pss[:, :], AF.Sqrt, [pssb], [rsb], scale=1.0 / 128, bias=EPS)
                recip(rs[:, 0:512], rs[:, 0:512], [rsb], [rsb])
                stt(yT[:, 8 + h, :], po[:, :], ogain, rs[:, 0:512], ALU.mult, ALU.mult, [pob, rsb, parb], [yTb[8 + h]])

            pre(0)
            dump("kT", kT[:, :, :], [128, 8, GT], BF16, kTb)
            dump("qT", yT[:, 8:16, :], [128, 8, GT], BF16, yTb[8:16])
            dump("it", it[:, :, :], [64, 8, 1024], BF16, [itb])
            dump("sT", sTt[0][:, :], [64, GT], BF16, [sTb[0]])
            dump("hmask", hmask[:, :, :], [64, 8, 64], BF16, [Bc])
            dump("eG", eG[:, :, :], [128, 4, GT], F32, eGb)
            for h in range(8):
                if h < 7:
                    pre(h + 1)
                post(h)
                if h == 0:
                    dump("y0", yT[:, 8, :], [128, GT], BF16, [yTb[8]])
            for b in range(2):
                w, wb_ = wacq(("ab_in", 0, 16, AB_IN_COLS[10 + b], 512))
                for m in range(4):
                    h = 4 * b + m
                    ps, pb = psum("mm")
                    proj_fm(w, wb_, 16, m, hT, hTb, ps, pb)
                    t1, t1b = tmp()
                    act(t1[:, 0:512], ps[:, :], AF.Silu, [pb], [t1b])
                    tt(yT[:, 8 + h, :], t1[:, 0:512], yT[:, 8 + h, :], ALU.mult, [t1b, yTb[8 + h]], [yTb[8 + h]])
            if dbg == "y0" and g == 0:
                S.dma("sp", dbg_y, yT[:, :, :], reads=yTb, writes=[], sbuf=yTb[0])
            out_proj("ab_out", yT, yTb)

        def out_proj(name, src, srcb):
            for cb in range(4):
                w, wb_ = wacq((name, 0, 16, cb * 512, 512))
                for m in range(4):
                    ps, pb = psum("mm")
                    proj_fm(w, wb_, 16, m, src, srcb, ps, pb)
                    c = 4 * cb + m
                    tt(xT[:, c, :], ps[:, :], xT[:, c, :], ALU.add, [pb, xTb[c]], [xTb[c]])

        def ffn(l):
            norm(gffn, 16 * l)
            up = "up%d" % l

            def conv(ps, pb, jj):
                acc, accb = tmp()
                zc, zcb = tmp()
                wcol = lambda j: cw[:, (l * 3 + j) * 88 + jj:(l * 3 + j) * 88 + jj + 1]
                act(acc[:, 0:512], ps[:, :], AF.Identity, [pb, parb], [accb], scale=wcol(2),
                    bias=cbias[:, l * 88 + jj:l * 88 + jj + 1])
                act(zc[:, 2:514], ps[:, :], AF.Copy, [pb], [zcb])
                cp(zc[:, 0:2], zhalo[l][:, jj, :], [zhb[l]], [zcb], eng="act")
                cp(zhalo[l][:, jj, :], zc[:, 512:514], [zcb], [zhb[l]], eng="act")
                stt(acc[:, 0:512], zc[:, 1:513], wcol(1), acc[:, 0:512], ALU.mult, ALU.add, [zcb, accb, parb], [accb])
                stt(acc[:, 0:512], zc[:, 0:512], wcol(0), acc[:, 0:512], ALU.mult, ALU.add, [zcb, accb, parb], [accb])
                return acc, accb

            for bi in range(22):
                wa, wab = wacq((up, 0, 16, (bi * 256, FH + bi * 256), 256))
                for m in range(2):
                    j = 2 * bi + m
                    psa, pba = psum("mm")
                    proj_fm(wa, wab, 16, m, hT, hTb, psa, pba)
                    psb_, pbb = psum("mm")
                    proj_fm(wa, wab, 16, 2 + m, hT, hTb, psb_, pbb)
                    aa, aab = conv(psa, pba, j)
                    bb, bbb = conv(psb_, pbb, 44 + j)
                    act(aa[:, 0:512], aa[:, 0:512], AF.Silu, [aab], [aab])
                    tt(actT[:, j, :], aa[:, 0:512], bb[:, 0:512], ALU.mult, [aab, bbb], [actb[j]])
            dn = "dn%d" % l
            for cb in range(4):
                accs = [psum("aux") for _ in range(4)]
                for kb in range(3):
                    nk = 16 if kb < 2 else 12
                    w, wb_ = wacq((dn, kb * 2048, nk, cb * 512, 512))
                    for m in range(4):
                        ps, pb = accs[m]
                        proj_fm(w, wb_, nk, m, actT, actb, ps, pb, k0=kb * 16, first=(kb == 0), last=(kb == 2))
                for m in range(4):
                    ps, pb = accs[m]
                    c = 4 * cb + m
                    tt(xT[:, c, :], ps[:, :], xT[:, c, :], ALU.add, [pb, xTb[c]], [xTb[c]])

        def qknorm(ps, pb, gain_ap, o, writes):
            sq, sqb_ = sqbuf()
            act(sq[:, :], ps[:, :], AF.Square, [pb], [sqb_])
            pss, pssb = psum("aux")
            mm(pss[:, :], ones_b[:, :], sq[:, :], True, True, [sqb_, Bc], pssb)
            rs, rsb = tmp()
            act(rs[:, 0:512], pss[:, :], AF.Sqrt, [pssb], [rsb], scale=1.0 / 128, bias=EPS)
            recip(rs[:, 0:512], rs[:, 0:512], [rsb], [rsb])
            stt(o, ps[:, :], gain_ap, rs[:, 0:512], ALU.mult, ALU.mult, [pb, rsb, parb], writes)

        def mixer_c(g):
            t0 = g * GT
            norm(gmix, 16)
            for b in range(4):
                w, wb_ = wacq(("c_in", 0, 16, b * 512, 512))
                for m in range(4):
                    h = 4 * b + m
                    ps, pb = psum("mm")
                    proj_fm(w, wb_, 16, m, hT, hTb, ps, pb)
                    qknorm(ps, pb, qgain, yT[:, h, :], [yTb[h]])
            for b in range(4):
                w, wb_ = wacq(("c_in", 0, 16, 2048 + b * 512, 512))
                for m in range(4):
                    h = 4 * b + m
                    ps, pb = psum("mm")
                    proj_fm(w, wb_, 16, m, hT, hTb, ps, pb)
                    sl = h % 2
                    qknorm(ps, pb, kgain, kst[sl][:, :], [kstb[sl]])
                    S.dma("sp", kT_d[h, :, t0:t0 + GT], kst[sl][:, :], reads=[kstb[sl]], writes=[kdb[h]], sbuf=kstb[sl])
            for b in range(4):
                w, wb_ = wacq(("c_in", 0, 16, 4096 + b * 512, 512))
                sl = b % 2
                for n in range(4):
                    ps, pb = psum("mm")
                    for kc in range(16):
                        mm(ps[:, :], hT[:, kc, n * 128:(n + 1) * 128], w[:, kc, :], kc == 0, kc == 15, [wb_, hTb[kc]], pb)
                    act(vst[sl][:, n, :], ps[:, :], AF.Copy, [pb], [vstb[sl]])
                for hh in range(4):
                    h = 4 * b + hh
                    S.dma("sp", v_d[h, :, 4 * g:4 * g + 4, :], vst[sl][:, :, hh * 128:(hh + 1) * 128],
                          reads=[vstb[sl]], writes=[vdb[h]], sbuf=vstb[sl])
            psf, psfb = psum("mm")
            for kc in range(16):
                mm(psf[0:16, :], wf[:, kc, :], hT[:, kc, :], kc == 0, kc == 15, [wfb, hTb[kc]], psfb)
            act(e16[:, :], psf[0:16, :], AF.Exp, [psfb, bfb], [e16b], scale=-1.0, bias=negbf[:, 0:1])
            act(e16[:, :], e16[:, :], AF.Ln, [e16b], [e16b], bias=1.0)
            S.op("dve", lambda e: e.tensor_tensor_scan(out=c16[:, :], data0=ones_f[0:16, :], data1=e16[:, :],
                                                       initial=ccarry[:, 0:1], op0=ALU.mult, op1=ALU.subtract),
                 reads=[e16b, Bc, ccb], writes=[c16b])
            cp(ccarry[:, 0:1], c16[:, 511:512], [c16b], [ccb])
            ptr, ptrb = psum("mm")
            for n in range(4):
                tr(ptr[:, n * 16:(n + 1) * 16], c16[0:16, n * 128:(n + 1) * 128], ident_f[0:16, 0:16], [c16b, Bc], ptrb)
            ts(negcT[:, 4 * g * 16:(4 * g + 4) * 16], ptr[:, 0:64], -1.0, ALU.mult, [ptrb], [negcb])

            nJ = 4 * g + 4
            npc = (nJ + 7) // 8
            pieces = [(h, pc) for h in range(16) for pc in range(npc)]
            kvs = {"loaded": 0}

            def kvload(i):
                h, pc = pieces[i]
                sl = i % NKV
                nb = min(8, nJ - pc * 8)
                S.dma("sp", kpc[sl][:, 0:nb * 128], kT_d[h, :, pc * 1024:pc * 1024 + nb * 128],
                      reads=[kdb[h]], writes=[kpcb[sl]], sbuf=kpcb[sl])
                S.dma("sp", vpc[sl][:, 0:nb, :], v_d[h, :, pc * 8:pc * 8 + nb, :],
                      reads=[vdb[h]], writes=[vpcb[sl]], sbuf=vpcb[sl])

            def kvacq(i):
                lim = min(len(pieces), i + NKV - 1)
                while kvs["loaded"] < lim:
                    kvload(kvs["loaded"])
                    kvs["loaded"] += 1
                return i % NKV

            scale = 128.0 ** -0.5
            pi = 0
            pcnt = 0
            for h in range(16):
                ts(e16[:, :], c16[:, :], ident_f[0:16, h:h + 1], ALU.mult, [c16b, Bc], [e16b])
                pcb_, pcbb = psum("mm")
                mm(pcb_[:, :], ones_f[0:16, 0:128], e16[:, :], True, True, [Bc, e16b], pcbb)
                cs_ = h % 2
                act(cbt[cs_][:, :], pcb_[:, :], AF.Copy, [pcbb], [cbtb[cs_]])
                O, Ob = psum("aux")
                L, Lb = psum("aux")
                slots = {}

                def qk(J):
                    pc = J // 8
                    if pc not in slots:
                        slots[pc] = kvacq(pi + pc)
                    sl = slots[pc]
                    lo = max(0, J - 4 * g) * 128
                    psc, pscb = psum("mm")
                    mm(psc[:, lo:512], kpc[sl][:, (J % 8) * 128:(J % 8 + 1) * 128], yT[:, h, lo:512], True, True,
                       [kpcb[sl], yTb[h]], pscb)
                    return psc, pscb, lo, sl

                cur = qk(0)
                for J in range(nJ):
                    psc, pscb, lo, sl = cur
                    tm, tmb = tmp()
                    stt(tm[:, lo:512], psc[:, lo:512], scale, cbt[cs_][:, lo:512], ALU.mult, ALU.add,
                        [pscb, cbtb[cs_]], [tmb])
                    if J >= 4 * g:
                        tt(tm[:, lo:lo + 128], tm[:, lo:lo + 128], maskneg[:, :], ALU.add, [tmb, Bc], [tmb])
                    pt, ptb = pTt[pcnt % 3], pTb[pcnt % 3]
                    pcnt += 1
                    act(pt[:, lo:512], tm[:, lo:512], AF.Exp, [tmb, negcb], [ptb],
                        bias=negcT[:, J * 16 + h:J * 16 + h + 1])
                    if J + 1 < nJ:
                        cur = qk(J + 1)
                    mm(O[:, lo:512], vpc[sl][:, J % 8, :], pt[:, lo:512], J == 0, J == nJ - 1, [vpcb[sl], ptb], Ob)
                    mm(L[:, lo:512], ones_b[:, :], pt[:, lo:512], J == 0, J == nJ - 1, [Bc, ptb], Lb)
                pi += npc
                rl, rlb = tmp()
                recip(rl[:, 0:512], L[:, :], [Lb], [rlb])
                tt(yT[:, h, :], O[:, :], rl[:, 0:512], ALU.mult, [Ob, rlb], [yTb[h]])
            for b in range(4):
                w, wb_ = wacq(("c_in", 0, 16, 6144 + b * 512, 512))
                for m in range(4):
                    h = 4 * b + m
                    ps, pb = psum("mm")
                    proj_fm(w, wb_, 16, m, hT, hTb, ps, pb)
                    t1, t1b = tmp()
                    act(t1[:, 0:512], ps[:, :], AF.Sigmoid, [pb], [t1b])
                    tt(yT[:, h, :], t1[:, 0:512], yT[:, h, :], ALU.mult, [t1b, yTb[h]], [yTb[h]])
            out_proj("c_out", yT, yTb)

        L0set = vnb + kTb + eGb + [itb] + ktokb
        FFNset = actb
        L1set = vstb + kpcb + vpcb + cbtb + kstb + pTb + [c16b, e16b]
        for g in range(NG):
            load_group(g)
            S.alias(FFNset, L0set)
            mixer_ab(g)
            if stop == "mix0":
                store_group(g)
                S.alias(L0set, FFNset)
                wstate["pos"] += len(gseq) - 16
                wstate["loaded"] = max(wstate["loaded"], wstate["pos"])
                continue
            S.alias(L0set, FFNset)
            ffn(0)
            if stop == "ffn0":
                store_group(g)
                wstate["pos"] += len(gseq) - 16 - 34
                wstate["loaded"] = max(wstate["loaded"], wstate["pos"])
                continue
            S.alias(FFNset, L1set)
            mixer_c(g)
            if stop == "mix1":
                store_group(g)
                S.alias(L1set, FFNset)
                wstate["pos"] += 34
                wstate["loaded"] = max(wstate["loaded"], wstate["pos"])
                continue
            S.alias(L1set, FFNset)
            ffn(1)
            store_group(g)
        S.emit(final_waits=xinb[0:1])
    return nc


def make_in_map(inp, b, NG=8):
    T = NG * GT
    f = lambda a: np.ascontiguousarray(np.asarray(a, dtype=np.float32))
    m = {
        "x": f(inp["x"][b, :T]),
        "mix_norm": f(inp["mix_norm"]).reshape(32, 128),
        "ffn_norm": f(inp["ffn_norm"]).reshape(32, 128),
        "ffn_conv_w": f(inp["ffn_conv_w"]).reshape(528, 128),
        "ffn_conv_b": f(inp["ffn_conv_b"]).reshape(176, 128),
        "hgrn_gamma": f(inp["hgrn_gamma"]).reshape(24, 128),
        "vecs": f(np.stack([np.asarray(inp["hgrn_o_norm"])[0], np.asarray(inp["c_q_norm"])[0],
                            np.asarray(inp["c_k_norm"])[0]], axis=0)),
        "ab_v_norm": f(inp["ab_v_norm"]).reshape(1024),
        "ab_sp_w": f(inp["ab_sp_w"]).reshape(8, 128, 128),
        "ab_sp_b": f(inp["ab_sp_b"]).reshape(1, 1024),
        "c_b_f": f(inp["c_b_f"]).reshape(16, 1),
        "ab_w_in": f(inp["ab_w_in"]).reshape(2048, 6144),
        "ab_w_out": f(inp["ab_w_out"]).reshape(2048, 2048),
        "ffn_w_up0": f(np.asarray(inp["ffn_w_up"])[0]),
        "ffn_w_up1": f(np.asarray(inp["ffn_w_up"])[1]),
        "ffn_w_down0": f(np.asarray(inp["ffn_w_down"])[0]),
        "ffn_w_down1": f(np.asarray(inp["ffn_w_down"])[1]),
        "c_w_in": f(inp["c_w_in"]).reshape(2048, 8208),
        "c_w_out": f(inp["c_w_out"]).reshape(2048, 2048),
    }
    return m


def kernel(**inputs):
    nc = build(8)
    B = inputs["x"].shape[0]
    in_maps = [make_in_map(inputs, b) for b in range(B)]
    res = run_bass_kernel_spmd(nc, in_maps, core_ids=list(range(B)))
    return np.stack([np.asarray(r["out"], dtype=np.float32) for r in res.results], axis=0)
```

```python
import numpy as np
import concourse.bass as bass
import concourse.mybir as mybir
from concourse.bass_utils import run_bass_kernel_spmd
from contextlib import ExitStack

F32 = mybir.dt.float32
BF16 = mybir.dt.bfloat16
AF = mybir.ActivationFunctionType
ALU = mybir.AluOpType
AX = mybir.AxisListType

ENGS = ("pe", "act", "dve", "pool", "sp")
EPS = 1e-6
D = 2048
FH = 2816
NFC = 22
NH0 = 4
NHC = 8
YC = 8
GT = 512
PAIRS = [[0, 1], [2, 3], [4, 5], [6, 7]]


class Buf:
    __slots__ = ("name", "last_write", "reads", "sem", "dcount")

    def __init__(self, name):
        self.name = name
        self.last_write = None
        self.reads = {}
        self.sem = None
        self.dcount = 0


def _tkey(t):
    return t[1] if t[0] == "c" else id(t[1])


class _Op:
    __slots__ = ("fn", "deps", "need_inc", "dma_buf", "val", "dma_inc")

    def __init__(self, fn, deps):
        self.fn = fn
        self.deps = deps
        self.need_inc = False
        self.dma_buf = None
        self.val = 0


class Sched:
    def __init__(self, nc):
        self.nc = nc
        self.ops = {e: [] for e in ENGS}
        self.dma_bufs = []
        self.same_eng_window = 2

    def _deps(self, eng, reads, writes, tok):
        deps = []
        raw = set()
        for b in reads:
            if b.last_write is not None:
                deps.append(b.last_write)
                raw.add(b.last_write)
        for b in writes:
            if b.last_write is not None:
                deps.append(b.last_write)
            deps.extend(b.reads.values())
        for b in reads:
            b.reads[_tkey(tok)] = tok
        for b in writes:
            b.last_write = tok
            b.reads = {}
        out = []
        seen = set()
        for d in deps:
            if d[0] == "c" and d[1] == eng:
                if not (d in raw and tok[0] == "c" and tok[2] - d[2] <= self.same_eng_window):
                    continue
            key = (d[0], d[1] if d[0] == "c" else id(d[1]), d[2])
            if key in seen:
                continue
            seen.add(key)
            out.append(d)
        return out

    def alias(self, old, new):
        toks = {}
        for b in old:
            cands = list(b.reads.values())
            if b.last_write is not None:
                cands.append(b.last_write)
            for t in cands:
                k = _tkey(t)
                if k not in toks or toks[k][2] < t[2]:
                    toks[k] = t
        for b in new:
            b.last_write = None
            b.reads = dict(toks)

    def op(self, eng, fn, reads=(), writes=()):
        idx = len(self.ops[eng])
        tok = ("c", eng, idx)
        deps = self._deps(eng, reads, writes, tok)
        self.ops[eng].append(_Op(fn, deps))
        return tok

    def dma(self, queue, out, in_, reads=(), writes=(), sbuf=None, slow=False):
        if slow:
            fn = lambda e: e.dma_start(out=out, in_=in_, allow_slow_non_contiguous=True)
        else:
            fn = lambda e: e.dma_start(out=out, in_=in_)
        return self.async_op(queue, fn, reads, writes, sbuf)

    def async_op(self, queue, fn, reads=(), writes=(), sbuf=None, inc=16):
        if sbuf.sem is None:
            self.dma_bufs.append(sbuf)
            sbuf.sem = True
        sbuf.dcount += inc
        tok = ("d", sbuf, sbuf.dcount)
        deps = self._deps(queue, reads, writes, tok)
        o = _Op(fn, deps)
        o.dma_inc = inc
        o.dma_buf = sbuf
        self.ops[queue].append(o)
        return tok

    def emit(self, final_waits=()):
        nc = self.nc
        for e in ENGS:
            for o in self.ops[e]:
                for d in o.deps:
                    if d[0] == "c":
                        self.ops[d[1]][d[2]].need_inc = True
        for e in ENGS:
            c = 0
            for o in self.ops[e]:
                if o.need_inc:
                    c += 1
                o.val = c
        with ExitStack() as st:
            esem = {e: st.enter_context(nc.semaphore("s_" + e)) for e in ENGS}
            for b in self.dma_bufs:
                b.sem = st.enter_context(nc.semaphore("d_" + b.name))
            block = st.enter_context(nc.Block())

            def run(ename, eng):
                known = {}
                for o in self.ops[ename]:
                    for d in o.deps:
                        if d[0] == "c":
                            sem = esem[d[1]]
                            v = self.ops[d[1]][d[2]].val
                            key = d[1]
                        else:
                            sem = d[1].sem
                            v = d[2]
                            key = id(d[1])
                        if known.get(key, 0) >= v:
                            continue
                        known[key] = v
                        eng.wait_ge(sem, v)
                    ins = o.fn(eng)
                    if o.dma_buf is not None:
                        ins.then_inc(o.dma_buf.sem, o.dma_inc)
                    elif o.need_inc:
                        ins.then_inc(esem[ename], 1)
                if ename == "sp":
                    for b in final_waits:
                        eng.wait_ge(b.sem, b.dcount)

            @block.tensor
            def _(eng):
                run("pe", eng)

            @block.scalar
            def _(eng):
                run("act", eng)

            @block.vector
            def _(eng):
                run("dve", eng)

            @block.gpsimd
            def _(eng):
                run("pool", eng)

            @block.sync
            def _(eng):
                run("sp", eng)


WSHAPES = {
    "ab_in": (2048, 3072), "ab_out": (1024, 2048), "up0": (2048, 5632), "dn0": (2816, 2048),
    "c_in": (2048, 4104), "c_out": (1024, 2048), "up1": (2048, 5632), "dn1": (2816, 2048),
}
AB_IN_COLS = [0, 512, 1024, 1536, 2048, 2560]


def weight_seq_group():
    seq = []
    for c0 in AB_IN_COLS:
        seq.append(("ab_in", 0, 16, c0, 512))
    for cb in range(4):
        seq.append(("ab_out", 0, YC, cb * 512, 512))

    def ffn(l):
        for bi in range(NFC // 2):
            seq.append(("up%d" % l, 0, 16, (bi * 256, FH + bi * 256), 256))
        for cb in range(4):
            for kb in range(2):
                seq.append(("dn%d" % l, kb * 2048, 16 if kb < 1 else NFC - 16, cb * 512, 512))
    ffn(0)
    for cb in range(8):
        seq.append(("c_in", 0, 16, cb * 512, 512))
    for cb in range(4):
        seq.append(("c_out", 0, YC, cb * 512, 512))
    ffn(1)
    return seq


def build(NG=8, stop=None, dbg=None):
    T = NG * GT
    nc = bass.Bass("TRN2", target_bir_lowering=False)

    def din(name, shape):
        return nc.dram_tensor(name, shape, F32, kind="ExternalInput").ap()

    x = din("x", [T, D])
    mix_norm = din("mix_norm", [32, 128])
    ffn_norm = din("ffn_norm", [32, 128])
    conv_w = din("ffn_conv_w", [264, 128])
    conv_b = din("ffn_conv_b", [88, 128])
    gamma = din("hgrn_gamma", [12, 128])
    vecs = din("vecs", [3, 128])
    v_norm = din("ab_v_norm", [512])
    sp_w = din("ab_sp_w", [NH0, 128, 128])
    sp_b = din("ab_sp_b", [1, 512])
    b_f = din("c_b_f", [NHC, 1])
    wf32 = {
        "ab_in": din("ab_w_in", list(WSHAPES["ab_in"])), "ab_out": din("ab_w_out", list(WSHAPES["ab_out"])),
        "up0": din("ffn_w_up0", list(WSHAPES["up0"])), "up1": din("ffn_w_up1", list(WSHAPES["up1"])),
        "dn0": din("ffn_w_down0", list(WSHAPES["dn0"])), "dn1": din("ffn_w_down1", list(WSHAPES["dn1"])),
        "c_in": din("c_w_in", list(WSHAPES["c_in"])), "c_out": din("c_w_out", list(WSHAPES["c_out"])),
    }
    out = nc.dram_tensor("out", [T, D], F32, kind="ExternalOutput").ap()
    dbg_y = nc.dram_tensor("dbg_y", [128, YC, GT], BF16, kind="ExternalOutput").ap() if dbg else None
    wbf = {k: nc.dram_tensor("wb_" + k, list(WSHAPES[k]), BF16).ap() for k in WSHAPES}
    kT_d = nc.dram_tensor("kT_d", [NHC, 128, T], BF16).ap()
    v_d = nc.dram_tensor("v_d", [NHC, 128, T // 128, 128], BF16).ap()
    cc_in = [nc.dram_tensor("cc_in%d" % i, [128, 2048], F32) for i in range(2)]
    cc_out = [nc.dram_tensor("cc_out%d" % i, [128, 2048], F32) for i in range(2)]

    S = Sched(nc)
    with ExitStack() as st:
        def sb(name, shape, dt):
            return st.enter_context(nc.sbuf_tensor(name, shape, dt))

        ident_f = sb("ident_f", [128, 128], F32)
        ident_b = sb("ident_b", [128, 128], BF16)
        ones_b = sb("ones_b", [128, 128], BF16)
        ones_f = sb("ones_f", [128, 512], F32)
        maskneg = sb("maskneg", [128, 128], F32)
        hmask = sb("hmask", [64, 8, 64], BF16)
        rmask = sb("rmask", [128, 512], F32)
        zhalo = [sb("zhalo%d" % l, [128, 2 * NFC, 2], F32) for l in range(2)]
        S32 = sb("S32", [128, NH0, 128], F32)
        ccarry = sb("ccarry", [NHC, 1], F32)
        negcT = sb("negcT", [128, (T // 128) * NHC], F32)
        gmix = sb("gmix", [128, 32], F32)
        gffn = sb("gffn", [128, 32], F32)
        cw = sb("cw", [128, 264], F32)
        cbias = sb("cbias", [128, 88], F32)
        gam = sb("gam", [128, 12], F32)
        lbt = sb("lbt", [128, NH0], F32)
        oml = sb("oml", [128, NH0], F32)
        vec3 = sb("vec3", [128, 3], F32)
        vgain = sb("vgain", [128, 512], F32)
        WcT = sb("WcT", [128, NH0, 128], BF16)
        spb = sb("spb", [1, 512], F32)
        negbf = sb("negbf", [NHC, 1], F32)
        wf = sb("wf", [128, 16, NHC], BF16)
        stg = sb("stg", [128, 128], F32)
        eGend = sb("eGend", [128, NH0 * 8], F32)
        xch = [sb("xch%d" % i, [128, 4, GT], F32) for i in range(2)]
        xin = [sb("xin0", [128, D], F32)] * 2
        xT = sb("xT", [128, 16, GT], F32)
        hT = sb("hT", [128, 16, GT], BF16)
        NWB = 2
        wt = [sb("wt%d" % i, [128, 16, 512], BF16) for i in range(NWB)]
        NSQ = 4
        sqs = [sb("sq%d" % i, [128, GT], BF16) for i in range(NSQ)]
        NTMP = 6
        tmps = [sb("tmp%d" % i, [128, 514], F32) for i in range(NTMP)]
        yT = sb("yT", [128, YC, GT], BF16)
        Sst = [sb("Sst%d" % i, [128, 8, 128], BF16) for i in range(2)]
        sTt = [sb("sT%d" % i, [64, GT], BF16) for i in range(2)]
        M3 = sb("M3", [128, 16896], BF16)
        actT = M3[:, 0:NFC * GT].rearrange("p (j t) -> p j t", t=GT)
        vn = M3[:, 0:2048].rearrange("p (n f) -> p n f", n=4)
        kT = M3[:, 2048:4096].rearrange("p (h t) -> p h t", h=NH0)
        eG = M3[:, 4096:8192].bitcast(F32).rearrange("p (m t) -> p m t", m=4)
        it = M3[0:64, 8192:12288].rearrange("p (c f) -> p c f", c=8)
        ktok = [M3[0:64, 12288 + i * 1024:12288 + (i + 1) * 1024].rearrange("p (c k) -> p c k", c=8) for i in range(2)]
        vst = [M3[:, i * 2048:(i + 1) * 2048].rearrange("p (n f) -> p n f", n=4) for i in range(2)]
        NKV = 3
        kpc = [M3[:, 4096 + i * 1024:4096 + (i + 1) * 1024] for i in range(NKV)]
        vpc = [M3[:, 7168 + i * 1024:7168 + (i + 1) * 1024].rearrange("p (j d) -> p j d", j=8) for i in range(NKV)]
        cbt = [M3[:, 10240 + i * 1024:10240 + (i + 1) * 1024].bitcast(F32) for i in range(2)]
        kst = [M3[:, 12288 + i * 512:12288 + (i + 1) * 512] for i in range(2)]
        pTt = [M3[:, 13312 + i * 512:13312 + (i + 1) * 512] for i in range(3)]
        c16 = M3[0:NHC, 14848:15872].bitcast(F32)
        e16 = M3[0:NHC, 15872:16896].bitcast(F32)
        psb = [st.enter_context(nc.psum_tensor("ps%d" % i, [128, 512], F32)) for i in range(8)]

        Bc = Buf("consts")
        PB = [Buf("ps%d" % i) for i in range(8)]
        xinb = [Buf("xin0")] * 2
        xTb = [Buf("xT%d" % c) for c in range(16)]
        hTb = [Buf("hT%d" % c) for c in range(16)]
        wtb = [Buf("wt%d" % i) for i in range(NWB)]
        sqb = [Buf("sq%d" % i) for i in range(NSQ)]
        tmpb = [Buf("tmp%d" % i) for i in range(NTMP)]
        yTb = [Buf("yT%d" % c) for c in range(YC)]
        vnb = [Buf("vn%d" % n) for n in range(4)]
        kTb = [Buf("kT%d" % h) for h in range(NH0)]
        eGb = [Buf("eG%d" % m) for m in range(4)]
        itb = Buf("it")
        ktokb = [Buf("ktok%d" % i) for i in range(2)]
        Sstb = [Buf("Sst%d" % i) for i in range(2)]
        sTb = [Buf("sT%d" % i) for i in range(2)]
        actb = [Buf("act%d" % j) for j in range(NFC)]
        xchb = [Buf("xch%d" % i) for i in range(2)]
        ccib = [Buf("cci%d" % i) for i in range(2)]
        ccob = [Buf("cco%d" % i) for i in range(2)]
        kstb = [Buf("kst%d" % i) for i in range(2)]
        vstb = [Buf("vst%d" % i) for i in range(2)]
        kpcb = [Buf("kpc%d" % i) for i in range(NKV)]
        vpcb = [Buf("vpc%d" % i) for i in range(NKV)]
        cbtb = [Buf("cbt%d" % i) for i in range(2)]
        pTb = [Buf("pT%d" % i) for i in range(3)]
        c16b = Buf("c16")
        e16b = Buf("e16")
        S32b = [Buf("S32_%d" % h) for h in range(NH0)]
        eGendb = Buf("eGend")
        zhb = [Buf("zh%d" % l) for l in range(2)]
        ccb = Buf("ccarry")
        negcb = Buf("negcT")
        stgb = Buf("stg")
        wfb = Buf("wf")
        parb = Buf("params")
        vgb = Buf("vgain")
        spbb = Buf("spb")
        bfb = Buf("bf")
        wmatb = {k: Buf("wm_" + k) for k in WSHAPES}
        kdb = [Buf("kd%d" % h) for h in range(NHC)]
        vdb = [Buf("vd%d" % h) for h in range(NHC)]

        dumps = {}

        def dump(name, ap, shape, dt, bufs):
            if dbg != "hg" or name in dumps:
                return
            dumps[name] = nc.dram_tensor("dbg_" + name, shape, dt, kind="ExternalOutput").ap()
            S.dma("sp", dumps[name], ap, reads=bufs, writes=[], sbuf=bufs[0])

        cnt = {"mm": 0, "aux": 0, "sq": 0, "tmp": 0}

        def psum(pool):
            if pool == "mm":
                i = cnt["mm"] % 4
                cnt["mm"] += 1
            else:
                i = 4 + cnt["aux"] % 4
                cnt["aux"] += 1
            return psb[i], PB[i]

        def sqbuf():
            i = cnt["sq"] % NSQ
            cnt["sq"] += 1
            return sqs[i], sqb[i]

        def tmp():
            i = cnt["tmp"] % NTMP
            cnt["tmp"] += 1
            return tmps[i], tmpb[i]

        def mm(o, lhsT, rhs, start, stop, reads, pb):
            S.op("pe", lambda e: e.matmul(o, lhsT=lhsT, rhs=rhs, start=start, stop=stop), reads=reads, writes=[pb])

        def tr(o, in_, ident, reads, pb):
            S.op("pe", lambda e: e.transpose(out=o, in_=in_, identity=ident), reads=reads, writes=[pb])

        def act(o, in_, func, reads, writes, **kw):
            S.op("act", lambda e: e.activation(out=o, in_=in_, func=func, **kw), reads=reads, writes=writes)

        def stt(o, in0, scalar, in1, op0, op1, reads, writes, eng="dve"):
            S.op(eng, lambda e: e.scalar_tensor_tensor(out=o, in0=in0, scalar=scalar, in1=in1, op0=op0, op1=op1),
                 reads=reads, writes=writes)

        def tt(o, in0, in1, op, reads, writes, eng="dve"):
            S.op(eng, lambda e: e.tensor_tensor(out=o, in0=in0, in1=in1, op=op), reads=reads, writes=writes)

        def ts(o, in0, s1, op0, reads, writes, s2=None, op1=None, eng="dve"):
            if op1 is None:
                S.op(eng, lambda e: e.tensor_scalar(out=o, in0=in0, scalar1=s1, scalar2=None, op0=op0),
                     reads=reads, writes=writes)
            else:
                S.op(eng, lambda e: e.tensor_scalar(out=o, in0=in0, scalar1=s1, scalar2=s2, op0=op0, op1=op1),
                     reads=reads, writes=writes)

        def cp(o, in_, reads, writes, eng="dve"):
            if eng == "act":
                S.op(eng, lambda e: e.activation(out=o, in_=in_, func=AF.Copy), reads=reads, writes=writes)
            else:
                S.op(eng, lambda e: e.tensor_copy(out=o, in_=in_), reads=reads, writes=writes)

        def recip(o, in_, reads, writes):
            S.op("dve", lambda e: e.reciprocal(out=o, in_=in_), reads=reads, writes=writes)

        def memset(ap, v, writes, eng="pool"):
            S.op(eng, lambda e: e.memset(ap, v), writes=writes)

        def asel(o, in_, pattern, cmp, fill, base, cm, reads, writes):
            S.op("pool", lambda e: e.affine_select(out=o, in_=in_, pattern=pattern, compare_op=cmp, fill=fill,
                                                   base=base, channel_multiplier=cm), reads=reads, writes=writes)

        memset(ident_f[:], 0.0, [Bc])
        asel(ident_f[:], ident_f[:], [[-1, 128]], ALU.not_equal, 1.0, 0, 1, [Bc], [Bc])
        cp(ident_b[:], ident_f[:], [Bc], [Bc], eng="pool")
        memset(ones_b[:], 1.0, [Bc])
        memset(ones_f[:], 1.0, [Bc])
        memset(maskneg[:], 0.0, [Bc])
        asel(maskneg[:], maskneg[:], [[1, 128]], ALU.is_ge, -30000.0, 0, -1, [Bc], [Bc])
        memset(hmask[:], 1.0, [Bc])
        asel(hmask[:], hmask[:], [[0, 8], [1, 64]], ALU.is_ge, 0.0, 0, -1, [Bc], [Bc])
        memset(rmask[:], 1.0, [Bc])
        memset(rmask[:].rearrange("p (c t) -> p c t", t=64)[:, :, 0:1], 0.0, [Bc])
        for l in range(2):
            memset(zhalo[l][:], 0.0, [zhb[l]])
        memset(S32[:], 0.0, S32b)
        memset(ccarry[:], 0.0, [ccb])

        for k in ["ab_in", "ab_out", "up0", "dn0", "c_in", "c_out", "up1", "dn1"]:
            rows = WSHAPES[k][0]
            step = 256
            for r0 in range(0, rows, step):
                S.dma("pool", wbf[k][r0:r0 + step, :], wf32[k][r0:r0 + step, :], writes=[wmatb[k]], sbuf=wmatb[k])

        def load_fm(src2d, nrows, dst_ap):
            S.dma("sp", stg[0:nrows, :], src2d, writes=[stgb], sbuf=stgb)
            ps, pb = psum("mm")
            tr(ps[:, 0:nrows], stg[0:nrows, :], ident_f[0:nrows, 0:nrows], [stgb, Bc], pb)
            cp(dst_ap, ps[:, 0:nrows], [pb], [parb])

        load_fm(mix_norm, 32, gmix[:, :])
        load_fm(ffn_norm, 32, gffn[:, :])
        for i in range(3):
            r0 = i * 128
            n = min(128, 264 - r0)
            load_fm(conv_w[r0:r0 + n, :], n, cw[:, r0:r0 + n])
        load_fm(conv_b[:, :], 88, cbias[:, :])
        load_fm(gamma, 12, gam[:, :])
        load_fm(vecs, 3, vec3[:, :])
        ogain = vec3[:, 0:1]
        qgain = vec3[:, 1:2]
        kgain = vec3[:, 2:3]
        act(gam[:, :], gam[:, :], AF.Exp, [parb], [parb])
        tt(lbt[:, :], gam[:, 0:4], gam[:, 4:8], ALU.add, [parb], [parb])
        tt(lbt[:, :], lbt[:, :], gam[:, 8:12], ALU.add, [parb], [parb])
        recip(lbt[:, :], lbt[:, :], [parb], [parb])
        tt(lbt[:, :], lbt[:, :], gam[:, 0:4], ALU.mult, [parb], [parb])
        ts(oml[:, :], lbt[:, :], -1.0, ALU.mult, [parb], [parb], s2=1.0, op1=ALU.add)
        for h in range(NH0):
            S.dma("sp", stg[:, :], sp_w[h], writes=[stgb], sbuf=stgb)
            ps, pb = psum("mm")
            tr(ps[:, 0:128], stg[:, :], ident_f[:, :], [stgb, Bc], pb)
            t_, tb_ = tmp()
            cp(t_[:, 0:128], ps[:, 0:128], [pb], [tb_])
            asel(t_[:, 0:128], t_[:, 0:128], [[1, 128]], ALU.is_ge, 0.0, 0, -1, [tb_], [tb_])
            cp(WcT[:, h, :], t_[:, 0:128], [tb_], [parb], eng="pool")
        S.dma("sp", vgain[:, :], v_norm.partition_broadcast(128), writes=[vgb], sbuf=vgb)
        S.dma("sp", spb[:, :], sp_b, writes=[spbb], sbuf=spbb)
        S.dma("sp", negbf[:, :], b_f, writes=[bfb], sbuf=bfb)
        ts(negbf[:, :], negbf[:, :], -1.0, ALU.mult, [bfb], [bfb])
        S.dma("sp", wf[:, :, :], wbf["c_in"][:, 4096:4096 + NHC].rearrange("(c p) n -> p c n", p=128),
              reads=[wmatb["c_in"]], writes=[wfb], sbuf=wfb, slow=True)

        gseq = weight_seq_group()
        wseq = gseq * NG
        wstate = {"pos": 0, "loaded": 0}

        def wload(i):
            name, r0, nk, c0, ncol = wseq[i]
            slot = i % NWB
            c0s = c0 if isinstance(c0, tuple) else (c0,)
            for ci, cc in enumerate(c0s):
                src = wbf[name][r0:r0 + nk * 128, cc:cc + ncol].rearrange("(c p) n -> p c n", p=128)
                S.dma("sp", wt[slot][:, 0:nk, ci * ncol:(ci + 1) * ncol], src, reads=[wmatb[name]],
                      writes=[wtb[slot]], sbuf=wtb[slot])

        def wacq(spec):
            i = wstate["pos"]
            assert wseq[i] == spec, (i, wseq[i], spec)
            lim = min(len(wseq), i + NWB)
            while wstate["loaded"] < lim:
                wload(wstate["loaded"])
                wstate["loaded"] += 1
            wstate["pos"] += 1
            s = i % NWB
            return wt[s], wtb[s]

        def load_group(g):
            t0 = g * GT
            for n in range(4):
                xi, xib = xin[n % 2], xinb[n % 2]
                S.dma("sp", xi[:, :], x[t0 + n * 128:t0 + (n + 1) * 128, :], writes=[xib], sbuf=xib)
                for q4 in range(4):
                    ps, pb = psum("mm")
                    for j in range(4):
                        c = 4 * q4 + j
                        tr(ps[:, j * 128:(j + 1) * 128], xi[:, c * 128:(c + 1) * 128], ident_f[:, :], [xib, Bc], pb)
                    cp(xT[:, 4 * q4:4 * q4 + 4, n * 128:(n + 1) * 128],
                       ps[:, :].rearrange("p (j t) -> p j t", j=4), [pb], xTb[4 * q4:4 * q4 + 4],
                       eng=("dve" if q4 % 2 == 0 else "act") if False else "dve")

        def store_group(g):
            t0 = g * GT
            for n in range(4):
                xi, xib = xin[n % 2], xinb[n % 2]
                for q4 in range(4):
                    ps, pb = psum("mm")
                    for j in range(4):
                        c = 4 * q4 + j
                        tr(ps[:, j * 128:(j + 1) * 128], xT[:, c, n * 128:(n + 1) * 128], ident_f[:, :], [xTb[c], Bc], pb)
                    S.op("act", lambda e, o=xi[:, q4 * 512:(q4 + 1) * 512], i_=ps[:, :]: e.activation(out=o, in_=i_, func=AF.Copy),
                         reads=[pb], writes=[xib])
                S.dma("sp", out[t0 + n * 128:t0 + (n + 1) * 128, :], xi[:, :], reads=[xib], writes=[], sbuf=xib)

        def norm(gt_, col0):
            ps, pb = psum("mm")
            for c in range(16):
                sq, sqb_ = sqbuf()
                act(sq[:, :], xT[:, c, :], AF.Square, [xTb[c]], [sqb_])
                mm(ps[:, :], ones_b[:, :], sq[:, :], c == 0, c == 15, [sqb_, Bc], pb)
            rs, rsb = tmp()
            act(rs[:, 0:512], ps[:, :], AF.Sqrt, [pb], [rsb], scale=1.0 / D, bias=EPS)
            recip(rs[:, 0:512], rs[:, 0:512], [rsb], [rsb])
            for c in range(16):
                stt(hT[:, c, :], xT[:, c, :], gt_[:, col0 + c:col0 + c + 1], rs[:, 0:512], ALU.mult, ALU.mult,
                    [xTb[c], rsb, parb], [hTb[c]])

        def proj_fm(w, wb_, nk, m, rhs_t, rhs_b, ps, pb, k0=0, first=True, last=True):
            for kc in range(nk):
                mm(ps[:, :], w[:, kc, m * 128:(m + 1) * 128], rhs_t[:, k0 + kc, :],
                   first and kc == 0, last and kc == nk - 1, [wb_, rhs_b[k0 + kc]], pb)

        def mixer_ab(g):
            norm(gmix, 0)
            for b in range(1):
                w, wb_ = wacq(("ab_in", 0, 16, AB_IN_COLS[0], 512))
                for m in range(4):
                    ps, pb = psum("mm")
                    proj_fm(w, wb_, 16, m, hT, hTb, ps, pb)
                    act(yT[:, 4 * b + m, :], ps[:, :], AF.Gelu, [pb], [yTb[4 * b + m]])
            for b in range(1):
                w, wb_ = wacq(("ab_in", 0, 16, AB_IN_COLS[1], 512))
                for n in range(4):
                    ps, pb = psum("mm")
                    for kc in range(16):
                        mm(ps[:, :], hT[:, kc, n * 128:(n + 1) * 128], w[:, kc, :], kc == 0, kc == 15, [wb_, hTb[kc]], pb)
                    vg, vgb_ = tmp()
                    act(vg[:, 0:512], ps[:, :], AF.Gelu, [pb], [vgb_])
                    s2, s2b = tmp()
                    tt(s2[:, 0:512], vg[:, 0:512], vg[:, 0:512], ALU.mult, [vgb_], [s2b])
                    r4, r4b = tmp()
                    S.op("dve", lambda e, o=r4[:, 0:4], i_=s2[:, 0:512].rearrange("p (h d) -> p h d", h=4):
                         e.tensor_reduce(out=o, in_=i_, axis=AX.X, op=ALU.add), reads=[s2b], writes=[r4b])
                    act(r4[:, 0:4], r4[:, 0:4], AF.Sqrt, [r4b], [r4b], scale=1.0 / 128, bias=EPS)
                    recip(r4[:, 0:4], r4[:, 0:4], [r4b], [r4b])
                    for hh in range(4):
                        h = 4 * b + hh
                        stt(vn[:, n, h * 128:(h + 1) * 128], vg[:, hh * 128:(hh + 1) * 128], r4[:, hh:hh + 1],
                            vgain[:, h * 128:(h + 1) * 128], ALU.mult, ALU.mult, [vgb_, r4b, vgb], [vnb[n]])
            for h in range(NH0):
                ps, pb = psum("mm")
                for n in range(4):
                    mm(ps[:, n * 128:(n + 1) * 128], vn[:, n, h * 128:(h + 1) * 128], WcT[:, h, :], True, False,
                       [vnb[n], parb], pb)
                    mm(ps[:, n * 128:(n + 1) * 128], ones_f[0:1, 0:128], spb[0:1, h * 128:(h + 1) * 128], False, True,
                       [Bc, spbb], pb)
                tt(yT[:, h, :], ps[:, :], yT[:, h, :], ALU.mult, [pb, yTb[h]], [yTb[h]])
            for b in range(1):
                w, wb_ = wacq(("ab_in", 0, 16, AB_IN_COLS[2], 512))
                for m in range(4):
                    h = 4 * b + m
                    ps, pb = psum("mm")
                    proj_fm(w, wb_, 16, m, hT, hTb, ps, pb)
                    t1, t1b = tmp()
                    act(t1[:, 0:512], ps[:, :], AF.Sigmoid, [pb], [t1b], scale=-1.0)
                    ts(t1[:, 0:512], t1[:, 0:512], oml[:, h:h + 1], ALU.mult, [t1b, parb], [t1b])
                    t2, t2b = tmp()
                    act(t2[:, 0:512], t1[:, 0:512], AF.Ln, [t1b], [t2b], scale=-1.0, bias=1.0)
                    G, Gb = tmp()
                    S.op("dve", lambda e, o=G[:, 0:512], d0=rmask[:, :], d1=t2[:, 0:512]:
                         e.tensor_tensor_scan(out=o, data0=d0, data1=d1, initial=0.0, op0=ALU.mult, op1=ALU.add),
                         reads=[t2b, Bc], writes=[Gb])
                    t3, t3b = tmp()
                    act(t3[:, 0:512], G[:, 0:512], AF.Exp, [Gb], [t3b], scale=-1.0)
                    tt(kT[:, h, :], t1[:, 0:512], t3[:, 0:512], ALU.mult, [t1b, t3b], [kTb[h]])
                    act(eG[:, m, :], G[:, 0:512], AF.Exp, [Gb], [eGb[m]])
                    cp(eGend[:, h * 8:(h + 1) * 8].rearrange("p (c o) -> p c o", o=1),
                       eG[:, m, :].rearrange("p (c t) -> p c t", t=64)[:, :, 63:64], [eGb[m]], [eGendb], eng="act")
                w, wb_ = wacq(("ab_in", 0, 16, AB_IN_COLS[3], 512))
                for m in range(4):
                    h = 4 * b + m
                    ps, pb = psum("mm")
                    proj_fm(w, wb_, 16, m, hT, hTb, ps, pb)
                    tt(yT[:, NH0 + h, :], ps[:, :], eG[:, m, :], ALU.mult, [pb, eGb[m]], [yTb[NH0 + h]])
            for b in range(1):
                w, wb_ = wacq(("ab_in", 0, 16, AB_IN_COLS[4], 512))
                for cch in range(8):
                    ps, pb = psum("mm")
                    for kc in range(16):
                        mm(ps[0:64, :], hT[:, kc, cch * 64:(cch + 1) * 64], w[:, kc, :], kc == 0, kc == 15,
                           [wb_, hTb[kc]], pb)
                    act(it[0:64, cch, b * 512:(b + 1) * 512], ps[0:64, :], AF.Copy, [pb], [itb])

            def pre(h):
                sl = h % 2
                pst, pstb = psum("mm")
                pbt = pst[:, :].bitcast(BF16)
                for cch in range(8):
                    tr(pbt[0:64, cch * 128:(cch + 1) * 128], kT[:, h, cch * 64:(cch + 1) * 64], ident_b[:, :],
                       [kTb[h], Bc], pstb)
                act(ktok[sl][0:64, :, :], pbt[0:64, 0:1024].rearrange("p (c k) -> p c k", c=8), AF.Copy, [pstb], [ktokb[sl]])
                pkv = [psum("aux"), psum("aux")]
                for cch in range(8):
                    pk, pkb = pkv[cch // 4]
                    mm(pk[:, (cch % 4) * 128:(cch % 4 + 1) * 128], ktok[sl][0:64, cch, :], it[0:64, cch, h * 128:(h + 1) * 128],
                       True, True, [ktokb[sl], itb], pkb)
                act(Sst[sl][:, 0, :], S32[:, h, :], AF.Copy, [S32b[h]], [Sstb[sl]])
                for cch in range(8):
                    pk, pkb = pkv[cch // 4]
                    tt(S32[:, h, :], S32[:, h, :], pk[:, (cch % 4) * 128:(cch % 4 + 1) * 128], ALU.add, [S32b[h], pkb], [S32b[h]])
                    ts(S32[:, h, :], S32[:, h, :], eGend[:, h * 8 + cch:h * 8 + cch + 1], ALU.mult, [S32b[h], eGendb], [S32b[h]])
                    if cch < 7:
                        act(Sst[sl][:, cch + 1, :], S32[:, h, :], AF.Copy, [S32b[h]], [Sstb[sl]])
                psc, pscb = psum("mm")
                for cch in range(8):
                    mm(psc[0:64, cch * 64:(cch + 1) * 64], kT[:, h, cch * 64:(cch + 1) * 64], yT[:, NH0 + h, cch * 64:(cch + 1) * 64],
                       True, True, [kTb[h], yTb[NH0 + h]], pscb)
                tt(sTt[sl][0:64, :], psc[0:64, :], hmask[0:64, :, :].rearrange("p c t -> p (c t)"), ALU.mult,
                   [pscb, Bc], [sTb[sl]])

            def post(h):
                sl = h % 2
                po, pob = psum("aux")
                for cch in range(8):
                    cs = slice(cch * 64, (cch + 1) * 64)
                    mm(po[:, cs], it[0:64, cch, h * 128:(h + 1) * 128], sTt[sl][0:64, cs], True, False, [itb, sTb[sl]], pob)
                    mm(po[:, cs], Sst[sl][:, cch, :], yT[:, NH0 + h, cs], False, True, [Sstb[sl], yTb[NH0 + h]], pob)
                sq, sqb_ = sqbuf()
                act(sq[:, :], po[:, :], AF.Square, [pob], [sqb_])
                pss, pssb = psum("mm")
                mm(pss[:, :], ones_b[:, :], sq[:, :], True, True, [sqb_, Bc], pssb)
                rs, rsb = tmp()
                act(rs[:, 0:512], pss[:, :], AF.Sqrt, [pssb], [rsb], scale=1.0 / 128, bias=EPS)
                recip(rs[:, 0:512], rs[:, 0:512], [rsb], [rsb])
                stt(yT[:, NH0 + h, :], po[:, :], ogain, rs[:, 0:512], ALU.mult, ALU.mult, [pob, rsb, parb], [yTb[NH0 + h]])

            pre(0)
            for h in range(NH0):
                if h < NH0 - 1:
                    pre(h + 1)
                post(h)
            for b in range(1):
                w, wb_ = wacq(("ab_in", 0, 16, AB_IN_COLS[5], 512))
                for m in range(4):
                    h = 4 * b + m
                    ps, pb = psum("mm")
                    proj_fm(w, wb_, 16, m, hT, hTb, ps, pb)
                    t1, t1b = tmp()
                    act(t1[:, 0:512], ps[:, :], AF.Silu, [pb], [t1b])
                    tt(yT[:, NH0 + h, :], t1[:, 0:512], yT[:, NH0 + h, :], ALU.mult, [t1b, yTb[NH0 + h]], [yTb[NH0 + h]])
            if dbg == "y0" and g == 0:
                S.dma("sp", dbg_y, yT[:, :, :], reads=yTb, writes=[], sbuf=yTb[0])
            out_proj("ab_out", yT, yTb)

        xcnt = {"n": 0}

        def allreduce_add(parts, cb):
            k = xcnt["n"] % 2
            xcnt["n"] += 1
            xc, xcb_ = xch[k], xchb[k]
            for m in range(4):
                ps, pb = parts[m]
                act(xc[:, m, :], ps[:, :], AF.Copy, [pb], [xcb_])
            S.dma("sp", cc_in[k].ap().rearrange("p (m t) -> p m t", m=4), xc[:, :, :], reads=[xcb_], writes=[ccib[k]], sbuf=xcb_)
            S.async_op("pool", lambda e, k=k: e.collective_compute("AllReduce", ALU.add, replica_groups=PAIRS,
                                                                 ins=[cc_in[k].ap().opt()], outs=[cc_out[k].ap().opt()]),
                       reads=[ccib[k]], writes=[ccob[k]], sbuf=ccob[k], inc=1)
            S.dma("sp", xc[:, :, :], cc_out[k].ap().rearrange("p (m t) -> p m t", m=4), reads=[ccob[k]], writes=[xcb_], sbuf=xcb_)
            for m in range(4):
                c = 4 * cb + m
                tt(xT[:, c, :], xc[:, m, :], xT[:, c, :], ALU.add, [xcb_, xTb[c]], [xTb[c]])

        def out_proj(name, src, srcb):
            for cb in range(4):
                w, wb_ = wacq((name, 0, YC, cb * 512, 512))
                parts = []
                for m in range(4):
                    ps, pb = psum("aux" if cb % 2 == 0 else "mm")
                    proj_fm(w, wb_, YC, m, src, srcb, ps, pb)
                    parts.append((ps, pb))
                allreduce_add(parts, cb)

        def ffn(l):
            norm(gffn, 16 * l)
            up = "up%d" % l

            def conv(ps, pb, jj):
                acc, accb = tmp()
                zc, zcb = tmp()

                wcol = lambda j: cw[:, (l * 3 + j) * 2 * NFC + jj:(l * 3 + j) * 2 * NFC + jj + 1]
                act(acc[:, 0:512], ps[:, :], AF.Identity, [pb, parb], [accb], scale=wcol(2),
                    bias=cbias[:, l * 2 * NFC + jj:l * 2 * NFC + jj + 1])
                act(zc[:, 2:514], ps[:, :], AF.Copy, [pb], [zcb])
                cp(zc[:, 0:2], zhalo[l][:, jj, :], [zhb[l]], [zcb], eng="act")
                cp(zhalo[l][:, jj, :], zc[:, 512:514], [zcb], [zhb[l]], eng="act")
                stt(acc[:, 0:512], zc[:, 1:513], wcol(1), acc[:, 0:512], ALU.mult, ALU.add, [zcb, accb, parb], [accb])
                stt(acc[:, 0:512], zc[:, 0:512], wcol(0), acc[:, 0:512], ALU.mult, ALU.add, [zcb, accb, parb], [accb])
                return acc, accb

            for bi in range(NFC // 2):
                wa, wab = wacq((up, 0, 16, (bi * 256, FH + bi * 256), 256))
                for m in range(2):
                    j = 2 * bi + m
                    psa, pba = psum("mm")
                    proj_fm(wa, wab, 16, m, hT, hTb, psa, pba)
                    psb_, pbb = psum("mm")
                    proj_fm(wa, wab, 16, 2 + m, hT, hTb, psb_, pbb)
                    aa, aab = conv(psa, pba, j)
                    bb, bbb = conv(psb_, pbb, NFC + j)
                    act(aa[:, 0:512], aa[:, 0:512], AF.Silu, [aab], [aab])
                    tt(actT[:, j, :], aa[:, 0:512], bb[:, 0:512], ALU.mult, [aab, bbb], [actb[j]])
            dn = "dn%d" % l
            for cb in range(4):
                accs = [psum("aux" if cb % 2 == 0 else "mm") for _ in range(4)]
                for kb in range(2):
                    nk = 16 if kb < 1 else NFC - 16
                    w, wb_ = wacq((dn, kb * 2048, nk, cb * 512, 512))
                    for m in range(4):
                        ps, pb = accs[m]
                        proj_fm(w, wb_, nk, m, actT, actb, ps, pb, k0=kb * 16, first=(kb == 0), last=(kb == 1))
                allreduce_add(accs, cb)

        def qknorm(ps, pb, gain_ap, o, writes):
            sq, sqb_ = sqbuf()
            act(sq[:, :], ps[:, :], AF.Square, [pb], [sqb_])
            pss, pssb = psum("aux")
            mm(pss[:, :], ones_b[:, :], sq[:, :], True, True, [sqb_, Bc], pssb)
            rs, rsb = tmp()
            act(rs[:, 0:512], pss[:, :], AF.Sqrt, [pssb], [rsb], scale=1.0 / 128, bias=EPS)
            recip(rs[:, 0:512], rs[:, 0:512], [rsb], [rsb])
            stt(o, ps[:, :], gain_ap, rs[:, 0:512], ALU.mult, ALU.mult, [pb, rsb, parb], writes)

        def mixer_c(g):
            t0 = g * GT
            norm(gmix, 16)
            for b in range(2):
                w, wb_ = wacq(("c_in", 0, 16, b * 512, 512))
                for m in range(4):
                    h = 4 * b + m
                    ps, pb = psum("mm")
                    proj_fm(w, wb_, 16, m, hT, hTb, ps, pb)
                    qknorm(ps, pb, qgain, yT[:, h, :], [yTb[h]])
            for b in range(2):
                w, wb_ = wacq(("c_in", 0, 16, 1024 + b * 512, 512))
                for m in range(4):
                    h = 4 * b + m
                    ps, pb = psum("mm")
                    proj_fm(w, wb_, 16, m, hT, hTb, ps, pb)
                    sl = h % 2
                    qknorm(ps, pb, kgain, kst[sl][:, :], [kstb[sl]])
                    S.dma("sp", kT_d[h, :, t0:t0 + GT], kst[sl][:, :], reads=[kstb[sl]], writes=[kdb[h]], sbuf=kstb[sl])
            for b in range(2):
                w, wb_ = wacq(("c_in", 0, 16, 2048 + b * 512, 512))
                sl = b % 2
                for n in range(4):
                    ps, pb = psum("mm")
                    for kc in range(16):
                        mm(ps[:, :], hT[:, kc, n * 128:(n + 1) * 128], w[:, kc, :], kc == 0, kc == 15, [wb_, hTb[kc]], pb)
                    act(vst[sl][:, n, :], ps[:, :], AF.Copy, [pb], [vstb[sl]])
                for hh in range(4):
                    h = 4 * b + hh
                    S.dma("sp", v_d[h, :, 4 * g:4 * g + 4, :], vst[sl][:, :, hh * 128:(hh + 1) * 128],
                          reads=[vstb[sl]], writes=[vdb[h]], sbuf=vstb[sl])
            psf, psfb = psum("mm")
            for kc in range(16):
                mm(psf[0:NHC, :], wf[:, kc, :], hT[:, kc, :], kc == 0, kc == 15, [wfb, hTb[kc]], psfb)
            act(e16[:, :], psf[0:NHC, :], AF.Exp, [psfb, bfb], [e16b], scale=-1.0, bias=negbf[:, 0:1])
            act(e16[:, :], e16[:, :], AF.Ln, [e16b], [e16b], bias=1.0)
            S.op("dve", lambda e: e.tensor_tensor_scan(out=c16[:, :], data0=ones_f[0:NHC, :], data1=e16[:, :],
                                                       initial=ccarry[:, 0:1], op0=ALU.mult, op1=ALU.subtract),
                 reads=[e16b, Bc, ccb], writes=[c16b])
            cp(ccarry[:, 0:1], c16[:, 511:512], [c16b], [ccb])
            ptr, ptrb = psum("mm")
            for n in range(4):
                tr(ptr[:, n * NHC:(n + 1) * NHC], c16[0:NHC, n * 128:(n + 1) * 128], ident_f[0:NHC, 0:NHC], [c16b, Bc], ptrb)
            ts(negcT[:, 4 * g * NHC:(4 * g + 4) * NHC], ptr[:, 0:4 * NHC], -1.0, ALU.mult, [ptrb], [negcb])

            nJ = 4 * g + 4
            npc = (nJ + 7) // 8
            pieces = [(h, pc) for h in range(NHC) for pc in range(npc)]
            kvs = {"loaded": 0}

            def kvload(i):
                h, pc = pieces[i]
                sl = i % NKV
                nb = min(8, nJ - pc * 8)
                S.dma("sp", kpc[sl][:, 0:nb * 128], kT_d[h, :, pc * 1024:pc * 1024 + nb * 128],
                      reads=[kdb[h]], writes=[kpcb[sl]], sbuf=kpcb[sl])
                S.dma("sp", vpc[sl][:, 0:nb, :], v_d[h, :, pc * 8:pc * 8 + nb, :],
                      reads=[vdb[h]], writes=[vpcb[sl]], sbuf=vpcb[sl])

            def kvacq(i):
                lim = min(len(pieces), i + NKV - 1)
                while kvs["loaded"] < lim:
                    kvload(kvs["loaded"])
                    kvs["loaded"] += 1
                return i % NKV

            scale = 128.0 ** -0.5
            pi = 0
            pcnt = 0
            for h in range(NHC):
                ts(e16[:, :], c16[:, :], ident_f[0:NHC, h:h + 1], ALU.mult, [c16b, Bc], [e16b])
                pcb_, pcbb = psum("mm")
                mm(pcb_[:, :], ones_f[0:NHC, 0:128], e16[:, :], True, True, [Bc, e16b], pcbb)
                cs_ = h % 2
                act(cbt[cs_][:, :], pcb_[:, :], AF.Copy, [pcbb], [cbtb[cs_]])
                O, Ob = psum("aux")
                L, Lb = psum("aux")
                slots = {}

                def qk(J):
                    pc = J // 8
                    if pc not in slots:
                        slots[pc] = kvacq(pi + pc)
                    sl = slots[pc]
                    lo = max(0, J - 4 * g) * 128
                    psc, pscb = psum("mm")
                    mm(psc[:, lo:512], kpc[sl][:, (J % 8) * 128:(J % 8 + 1) * 128], yT[:, h, lo:512], True, True,
                       [kpcb[sl], yTb[h]], pscb)
                    return psc, pscb, lo, sl

                cur = qk(0)
                for J in range(nJ):
                    psc, pscb, lo, sl = cur
                    tm, tmb = tmp()
                    stt(tm[:, lo:512], psc[:, lo:512], scale, cbt[cs_][:, lo:512], ALU.mult, ALU.add,
                        [pscb, cbtb[cs_]], [tmb])
                    if J >= 4 * g:
                        tt(tm[:, lo:lo + 128], tm[:, lo:lo + 128], maskneg[:, :], ALU.add, [tmb, Bc], [tmb])
                    pt, ptb = pTt[pcnt % 3], pTb[pcnt % 3]
                    pcnt += 1
                    act(pt[:, lo:512], tm[:, lo:512], AF.Exp, [tmb, negcb], [ptb],
                        bias=negcT[:, J * NHC + h:J * NHC + h + 1])
                    if J + 1 < nJ:
                        cur = qk(J + 1)
                    mm(O[:, lo:512], vpc[sl][:, J % 8, :], pt[:, lo:512], J == 0, J == nJ - 1, [vpcb[sl], ptb], Ob)
                    mm(L[:, lo:512], ones_b[:, :], pt[:, lo:512], J == 0, J == nJ - 1, [Bc, ptb], Lb)
                pi += npc
                rl, rlb = tmp()
                recip(rl[:, 0:512], L[:, :], [Lb], [rlb])
                tt(yT[:, h, :], O[:, :], rl[:, 0:512], ALU.mult, [Ob, rlb], [yTb[h]])
            for b in range(2):
                w, wb_ = wacq(("c_in", 0, 16, 3072 + b * 512, 512))
                for m in range(4):
                    h = 4 * b + m
                    ps, pb = psum("mm")
                    proj_fm(w, wb_, 16, m, hT, hTb, ps, pb)
                    t1, t1b = tmp()
                    act(t1[:, 0:512], ps[:, :], AF.Sigmoid, [pb], [t1b])
                    tt(yT[:, h, :], t1[:, 0:512], yT[:, h, :], ALU.mult, [t1b, yTb[h]], [yTb[h]])
            out_proj("c_out", yT, yTb)

        L0set = vnb + kTb + eGb + [itb] + ktokb
        FFNset = actb
        L1set = vstb + kpcb + vpcb + cbtb + kstb + pTb + [c16b, e16b]
        for g in range(NG):
            load_group(g)
            S.alias(FFNset, L0set)
            mixer_ab(g)
            if stop == "mix0":
                store_group(g)
                S.alias(L0set, FFNset)
                wstate["pos"] += len(gseq) - 10
                wstate["loaded"] = max(wstate["loaded"], wstate["pos"])
                continue
            S.alias(L0set, FFNset)
            ffn(0)
            if stop == "ffn0":
                store_group(g)
                wstate["pos"] += len(gseq) - 10 - 19
                wstate["loaded"] = max(wstate["loaded"], wstate["pos"])
                continue
            S.alias(FFNset, L1set)
            mixer_c(g)
            if stop == "mix1":
                store_group(g)
                S.alias(L1set, FFNset)
                wstate["pos"] += 19
                wstate["loaded"] = max(wstate["loaded"], wstate["pos"])
                continue
            S.alias(L1set, FFNset)
            ffn(1)
            store_group(g)
        S.emit(final_waits=xinb[0:1])
    return nc


def make_in_map(inp, b, j, NG=8):
    T = NG * GT
    A = lambda k: np.asarray(inp[k], dtype=np.float32)
    f = lambda a: np.ascontiguousarray(a, dtype=np.float32)
    hs = slice(j * 512, (j + 1) * 512)
    w_in = A("ab_w_in")[0]
    ab_in = np.concatenate([w_in[:, 0 + j * 512:0 + (j + 1) * 512],
                            w_in[:, 1024 + j * 512:1024 + (j + 1) * 512],
                            w_in[:, 3072 + j * 512:3072 + (j + 1) * 512],
                            w_in[:, 2048 + j * 512:2048 + (j + 1) * 512],
                            w_in[:, 4096 + j * 512:4096 + (j + 1) * 512],
                            w_in[:, 5120 + j * 512:5120 + (j + 1) * 512]], axis=1)
    w_out = A("ab_w_out")[0]
    ab_out = np.concatenate([w_out[hs], w_out[1024 + j * 512:1024 + (j + 1) * 512]], axis=0)
    fs = slice(j * FH, (j + 1) * FH)
    fs2 = slice(5632 + j * FH, 5632 + (j + 1) * FH)
    up = A("ffn_w_up")
    cwv = A("ffn_conv_w")
    cbv = A("ffn_conv_b")
    c_in = A("c_w_in")[0]
    cs = slice(j * 1024, (j + 1) * 1024)
    c_in_h = np.concatenate([c_in[:, 0 + j * 1024:0 + (j + 1) * 1024], c_in[:, 2048 + j * 1024:2048 + (j + 1) * 1024],
                             c_in[:, 4096 + j * 1024:4096 + (j + 1) * 1024], c_in[:, 6144 + j * 1024:6144 + (j + 1) * 1024],
                             c_in[:, 8192 + j * 8:8192 + (j + 1) * 8]], axis=1)
    m = {
        "x": f(A("x")[b, :T]),
        "mix_norm": f(A("mix_norm").reshape(32, 128)),
        "ffn_norm": f(A("ffn_norm").reshape(32, 128)),
        "ffn_conv_w": f(np.concatenate([cwv[:, :, fs], cwv[:, :, fs2]], axis=2).reshape(264, 128)),
        "ffn_conv_b": f(np.concatenate([cbv[:, fs], cbv[:, fs2]], axis=1).reshape(88, 128)),
        "hgrn_gamma": f(A("hgrn_gamma")[:, hs].reshape(12, 128)),
        "vecs": f(np.stack([A("hgrn_o_norm")[0], A("c_q_norm")[0], A("c_k_norm")[0]], axis=0)),
        "ab_v_norm": f(A("ab_v_norm")[0, hs]),
        "ab_sp_w": f(A("ab_sp_w")[0, 4 * j:4 * j + 4]),
        "ab_sp_b": f(A("ab_sp_b")[0, 4 * j:4 * j + 4].reshape(1, 512)),
        "c_b_f": f(A("c_b_f")[0, 8 * j:8 * j + 8].reshape(8, 1)),
        "ab_w_in": f(ab_in),
        "ab_w_out": f(ab_out),
        "ffn_w_up0": f(np.concatenate([up[0][:, fs], up[0][:, fs2]], axis=1)),
        "ffn_w_up1": f(np.concatenate([up[1][:, fs], up[1][:, fs2]], axis=1)),
        "ffn_w_down0": f(A("ffn_w_down")[0][fs]),
        "ffn_w_down1": f(A("ffn_w_down")[1][fs]),
        "c_w_in": f(c_in_h),
        "c_w_out": f(A("c_w_out")[0][cs]),
    }
    return m


def kernel(**inputs):
    nc = build(8)
    B = inputs["x"].shape[0]
    in_maps = [make_in_map(inputs, c // 2, c % 2) for c in range(2 * B)]
    res = run_bass_kernel_spmd(nc, in_maps, core_ids=list(range(2 * B)))
    return np.stack([np.asarray(res.results[2 * b]["out"], dtype=np.float32) for b in range(B)], axis=0)
```

```python
import numpy as np
import concourse.bass as bass
import concourse.mybir as mybir
from concourse.bass_utils import run_bass_kernel_spmd
from contextlib import ExitStack

F32 = mybir.dt.float32
BF16 = mybir.dt.bfloat16
AF = mybir.ActivationFunctionType
ALU = mybir.AluOpType
AX = mybir.AxisListType

ENGS = ("pe", "act", "dve", "pool", "sp")
EPS = 1e-6
D = 2048
FH = 2816
NFC = 22
NH0 = 4
NHC = 8
YC = 8
GT = 512
PAIRS = [[0, 1], [2, 3], [4, 5], [6, 7]]


class Buf:
    __slots__ = ("name", "last_write", "reads", "sem", "dcount")

    def __init__(self, name):
        self.name = name
        self.last_write = None
        self.reads = {}
        self.sem = None
        self.dcount = 0


def _tkey(t):
    return t[1] if t[0] == "c" else id(t[1])


class _Op:
    __slots__ = ("fn", "deps", "need_inc", "dma_buf", "val", "dma_inc")

    def __init__(self, fn, deps):
        self.fn = fn
        self.deps = deps
        self.need_inc = False
        self.dma_buf = None
        self.val = 0


class Sched:
    def __init__(self, nc):
        self.nc = nc
        self.ops = {e: [] for e in ENGS}
        self.dma_bufs = []
        self.same_eng_window = 2

    def _deps(self, eng, reads, writes, tok):
        deps = []
        raw = set()
        for b in reads:
            if b.last_write is not None:
                deps.append(b.last_write)
                raw.add(b.last_write)
        for b in writes:
            if b.last_write is not None:
                deps.append(b.last_write)
            deps.extend(b.reads.values())
        for b in reads:
            b.reads[_tkey(tok)] = tok
        for b in writes:
            b.last_write = tok
            b.reads = {}
        out = []
        seen = set()
        for d in deps:
            if d[0] == "c" and d[1] == eng:
                if not (d in raw and tok[0] == "c" and tok[2] - d[2] <= self.same_eng_window):
                    continue
            key = (d[0], d[1] if d[0] == "c" else id(d[1]), d[2])
            if key in seen:
                continue
            seen.add(key)
            out.append(d)
        return out

    def alias(self, old, new):
        toks = {}
        for b in old:
            cands = list(b.reads.values())
            if b.last_write is not None:
                cands.append(b.last_write)
            for t in cands:
                k = _tkey(t)
                if k not in toks or toks[k][2] < t[2]:
                    toks[k] = t
        for b in new:
            b.last_write = None
            b.reads = dict(toks)

    def op(self, eng, fn, reads=(), writes=()):
        idx = len(self.ops[eng])
        tok = ("c", eng, idx)
        deps = self._deps(eng, reads, writes, tok)
        self.ops[eng].append(_Op(fn, deps))
        return tok

    def dma(self, queue, out, in_, reads=(), writes=(), sbuf=None, slow=False):
        if slow:
            fn = lambda e: e.dma_start(out=out, in_=in_, allow_slow_non_contiguous=True)
        else:
            fn = lambda e: e.dma_start(out=out, in_=in_)
        return self.async_op(queue, fn, reads, writes, sbuf)

    def async_op(self, queue, fn, reads=(), writes=(), sbuf=None, inc=16):
        if sbuf.sem is None:
            self.dma_bufs.append(sbuf)
            sbuf.sem = True
        sbuf.dcount += inc
        tok = ("d", sbuf, sbuf.dcount)
        deps = self._deps(queue, reads, writes, tok)
        o = _Op(fn, deps)
        o.dma_inc = inc
        o.dma_buf = sbuf
        self.ops[queue].append(o)
        return tok

    def emit(self, final_waits=()):
        nc = self.nc
        for e in ENGS:
            for o in self.ops[e]:
                for d in o.deps:
                    if d[0] == "c":
                        self.ops[d[1]][d[2]].need_inc = True
        for e in ENGS:
            c = 0
            for o in self.ops[e]:
                if o.need_inc:
                    c += 1
                o.val = c
        with ExitStack() as st:
            esem = {e: st.enter_context(nc.semaphore("s_" + e)) for e in ENGS}
            for b in self.dma_bufs:
                b.sem = st.enter_context(nc.semaphore("d_" + b.name))
            block = st.enter_context(nc.Block())

            def run(ename, eng):
                known = {}
                for o in self.ops[ename]:
                    for d in o.deps:
                        if d[0] == "c":
                            sem = esem[d[1]]
                            v = self.ops[d[1]][d[2]].val
                            key = d[1]
                        else:
                            sem = d[1].sem
                            v = d[2]
                            key = id(d[1])
                        if known.get(key, 0) >= v:
                            continue
                        known[key] = v
                        eng.wait_ge(sem, v)
                    ins = o.fn(eng)
                    if o.dma_buf is not None:
                        ins.then_inc(o.dma_buf.sem, o.dma_inc)
                    elif o.need_inc:
                        ins.then_inc(esem[ename], 1)
                if ename == "sp":
                    for b in final_waits:
                        eng.wait_ge(b.sem, b.dcount)

            @block.tensor
            def _(eng):
                run("pe", eng)

            @block.scalar
            def _(eng):
                run("act", eng)

            @block.vector
            def _(eng):
                run("dve", eng)

            @block.gpsimd
            def _(eng):
                run("pool", eng)

            @block.sync
            def _(eng):
                run("sp", eng)


WSHAPES = {
    "ab_in": (2048, 3072), "ab_out": (1024, 2048), "up0": (2048, 5632), "dn0": (2816, 2048),
    "c_in": (2048, 4104), "c_out": (1024, 2048), "up1": (2048, 5632), "dn1": (2816, 2048),
}
AB_IN_COLS = [0, 512, 1024, 1536, 2048, 2560]


def weight_seq_group():
    seq = []
    for c0 in AB_IN_COLS:
        seq.append(("ab_in", 0, 16, c0, 512))
    for cb in range(4):
        seq.append(("ab_out", 0, YC, cb * 512, 512))

    def ffn(l):
        for bi in range(NFC // 2):
            seq.append(("up%d" % l, 0, 16, (bi * 256, FH + bi * 256), 256))
        for cb in range(4):
            for kb in range(2):
                seq.append(("dn%d" % l, kb * 2048, 16 if kb < 1 else NFC - 16, cb * 512, 512))
    ffn(0)
    for cb in range(8):
        seq.append(("c_in", 0, 16, cb * 512, 512))
    for cb in range(4):
        seq.append(("c_out", 0, YC, cb * 512, 512))
    ffn(1)
    return seq


def build(NG=8, stop=None, dbg=None):
    T = NG * GT
    nc = bass.Bass("TRN2", target_bir_lowering=False)

    def din(name, shape):
        return nc.dram_tensor(name, shape, F32, kind="ExternalInput").ap()

    x = din("x", [T, D])
    mix_norm = din("mix_norm", [32, 128])
    ffn_norm = din("ffn_norm", [32, 128])
    conv_w = din("ffn_conv_w", [264, 128])
    conv_b = din("ffn_conv_b", [88, 128])
    gamma = din("hgrn_gamma", [12, 128])
    vecs = din("vecs", [3, 128])
    v_norm = din("ab_v_norm", [512])
    sp_w = din("ab_sp_w", [NH0, 128, 128])
    sp_b = din("ab_sp_b", [1, 512])
    b_f = din("c_b_f", [NHC, 1])
    wf32 = {
        "ab_in": din("ab_w_in", list(WSHAPES["ab_in"])), "ab_out": din("ab_w_out", list(WSHAPES["ab_out"])),
        "up0": din("ffn_w_up0", list(WSHAPES["up0"])), "up1": din("ffn_w_up1", list(WSHAPES["up1"])),
        "dn0": din("ffn_w_down0", list(WSHAPES["dn0"])), "dn1": din("ffn_w_down1", list(WSHAPES["dn1"])),
        "c_in": din("c_w_in", list(WSHAPES["c_in"])), "c_out": din("c_w_out", list(WSHAPES["c_out"])),
    }
    out = nc.dram_tensor("out", [T, D], F32, kind="ExternalOutput").ap()
    dbg_y = nc.dram_tensor("dbg_y", [128, YC, GT], BF16, kind="ExternalOutput").ap() if dbg else None
    wbf = {k: nc.dram_tensor("wb_" + k, list(WSHAPES[k]), BF16).ap() for k in WSHAPES}
    kT_d = nc.dram_tensor("kT_d", [NHC, 128, T], BF16).ap()
    v_d = nc.dram_tensor("v_d", [NHC, 128, T // 128, 128], BF16).ap()
    NCC = 4
    cc_in = [nc.dram_tensor("cc_in%d" % i, [128, 2048], F32) for i in range(NCC)]
    cc_out = [nc.dram_tensor("cc_out%d" % i, [128, 2048], F32) for i in range(NCC)]

    S = Sched(nc)
    with ExitStack() as st:
        def sb(name, shape, dt):
            return st.enter_context(nc.sbuf_tensor(name, shape, dt))

        ident_f = sb("ident_f", [128, 128], F32)
        ident_b = sb("ident_b", [128, 128], BF16)
        ones_b = sb("ones_b", [128, 128], BF16)
        ones_f = sb("ones_f", [128, 512], F32)
        maskneg = sb("maskneg", [128, 128], F32)
        hmask = sb("hmask", [64, 8, 64], BF16)
        rmask = sb("rmask", [128, 512], F32)
        zhalo = [sb("zhalo%d" % l, [128, 2 * NFC, 2], F32) for l in range(2)]
        S32 = sb("S32", [128, NH0, 128], F32)
        ccarry = sb("ccarry", [NHC, 1], F32)
        negcT = sb("negcT", [128, (T // 128) * NHC], F32)
        gmix = sb("gmix", [128, 32], F32)
        gffn = sb("gffn", [128, 32], F32)
        cw = sb("cw", [128, 264], F32)
        cbias = sb("cbias", [128, 88], F32)
        gam = sb("gam", [128, 12], F32)
        lbt = sb("lbt", [128, NH0], F32)
        oml = sb("oml", [128, NH0], F32)
        vec3 = sb("vec3", [128, 3], F32)
        vgain = sb("vgain", [128, 512], F32)
        WcT = sb("WcT", [128, NH0, 128], BF16)
        spb = sb("spb", [1, 512], F32)
        negbf = sb("negbf", [NHC, 1], F32)
        wf = sb("wf", [128, 16, NHC], BF16)
        stg = sb("stg", [128, 128], F32)
        eGend = sb("eGend", [128, NH0 * 8], F32)
        xch = [sb("xch%d" % i, [128, 4, GT], F32) for i in range(2)]
        xin = [sb("xin0", [128, D], F32)] * 2
        xT = sb("xT", [128, 16, GT], F32)
        hT = sb("hT", [128, 16, GT], BF16)
        NWB = 2
        wt = [sb("wt%d" % i, [128, 16, 512], BF16) for i in range(NWB)]
        NSQ = 4
        sqs = [sb("sq%d" % i, [128, GT], BF16) for i in range(NSQ)]
        NTMP = 6
        tmps = [sb("tmp%d" % i, [128, 514], F32) for i in range(NTMP)]
        yT = sb("yT", [128, YC, GT], BF16)
        Sst = [sb("Sst%d" % i, [128, 8, 128], BF16) for i in range(2)]
        sTt = [sb("sT%d" % i, [64, GT], BF16) for i in range(2)]
        M3 = sb("M3", [128, 16896], BF16)
        actT = M3[:, 0:NFC * GT].rearrange("p (j t) -> p j t", t=GT)
        vn = M3[:, 0:2048].rearrange("p (n f) -> p n f", n=4)
        kT = M3[:, 2048:4096].rearrange("p (h t) -> p h t", h=NH0)
        eG = M3[:, 4096:8192].bitcast(F32).rearrange("p (m t) -> p m t", m=4)
        it = M3[0:64, 8192:12288].rearrange("p (c f) -> p c f", c=8)
        ktok = [M3[0:64, 12288 + i * 1024:12288 + (i + 1) * 1024].rearrange("p (c k) -> p c k", c=8) for i in range(2)]
        vst = [M3[:, i * 2048:(i + 1) * 2048].rearrange("p (n f) -> p n f", n=4) for i in range(2)]
        NKV = 3
        kpc = [M3[:, 4096 + i * 1024:4096 + (i + 1) * 1024] for i in range(NKV)]
        vpc = [M3[:, 7168 + i * 1024:7168 + (i + 1) * 1024].rearrange("p (j d) -> p j d", j=8) for i in range(NKV)]
        cbt = [M3[:, 10240 + i * 1024:10240 + (i + 1) * 1024].bitcast(F32) for i in range(2)]
        kst = [M3[:, 12288 + i * 512:12288 + (i + 1) * 512] for i in range(2)]
        pTt = [M3[:, 13312 + i * 512:13312 + (i + 1) * 512] for i in range(3)]
        c16 = M3[0:NHC, 14848:15872].bitcast(F32)
        e16 = M3[0:NHC, 15872:16896].bitcast(F32)
        psb = [st.enter_context(nc.psum_tensor("ps%d" % i, [128, 512], F32)) for i in range(8)]

        Bc = Buf("consts")
        PB = [Buf("ps%d" % i) for i in range(8)]
        xinb = [Buf("xin0")] * 2
        xTb = [Buf("xT%d" % c) for c in range(16)]
        hTb = [Buf("hT%d" % c) for c in range(16)]
        wtb = [Buf("wt%d" % i) for i in range(NWB)]
        sqb = [Buf("sq%d" % i) for i in range(NSQ)]
        tmpb = [Buf("tmp%d" % i) for i in range(NTMP)]
        yTb = [Buf("yT%d" % c) for c in range(YC)]
        vnb = [Buf("vn%d" % n) for n in range(4)]
        kTb = [Buf("kT%d" % h) for h in range(NH0)]
        eGb = [Buf("eG%d" % m) for m in range(4)]
        itb = Buf("it")
        ktokb = [Buf("ktok%d" % i) for i in range(2)]
        Sstb = [Buf("Sst%d" % i) for i in range(2)]
        sTb = [Buf("sT%d" % i) for i in range(2)]
        actb = [Buf("act%d" % j) for j in range(NFC)]
        xchb = [Buf("xch%d" % i) for i in range(2)]
        ccib = [Buf("cci%d" % i) for i in range(NCC)]
        ccob = [Buf("cco%d" % i) for i in range(NCC)]
        accb = [Buf("acc%d" % i) for i in range(NCC)]
        kstb = [Buf("kst%d" % i) for i in range(2)]
        vstb = [Buf("vst%d" % i) for i in range(2)]
        kpcb = [Buf("kpc%d" % i) for i in range(NKV)]
        vpcb = [Buf("vpc%d" % i) for i in range(NKV)]
        cbtb = [Buf("cbt%d" % i) for i in range(2)]
        pTb = [Buf("pT%d" % i) for i in range(3)]
        c16b = Buf("c16")
        e16b = Buf("e16")
        S32b = [Buf("S32_%d" % h) for h in range(NH0)]
        eGendb = Buf("eGend")
        zhb = [Buf("zh%d" % l) for l in range(2)]
        ccb = Buf("ccarry")
        negcb = Buf("negcT")
        stgb = Buf("stg")
        wfb = Buf("wf")
        parb = Buf("params")
        vgb = Buf("vgain")
        spbb = Buf("spb")
        bfb = Buf("bf")
        wmatb = {k: Buf("wm_" + k) for k in WSHAPES}
        kdb = [Buf("kd%d" % h) for h in range(NHC)]
        vdb = [Buf("vd%d" % h) for h in range(NHC)]

        dumps = {}

        def dump(name, ap, shape, dt, bufs):
            if dbg != "hg" or name in dumps:
                return
            dumps[name] = nc.dram_tensor("dbg_" + name, shape, dt, kind="ExternalOutput").ap()
            S.dma("sp", dumps[name], ap, reads=bufs, writes=[], sbuf=bufs[0])

        cnt = {"mm": 0, "aux": 0, "sq": 0, "tmp": 0}

        def psum(pool):
            if pool == "mm":
                i = cnt["mm"] % 4
                cnt["mm"] += 1
            else:
                i = 4 + cnt["aux"] % 4
                cnt["aux"] += 1
            return psb[i], PB[i]

        def sqbuf():
            i = cnt["sq"] % NSQ
            cnt["sq"] += 1
            return sqs[i], sqb[i]

        def tmp():
            i = cnt["tmp"] % NTMP
            cnt["tmp"] += 1
            return tmps[i], tmpb[i]

        def mm(o, lhsT, rhs, start, stop, reads, pb):
            S.op("pe", lambda e: e.matmul(o, lhsT=lhsT, rhs=rhs, start=start, stop=stop), reads=reads, writes=[pb])

        def tr(o, in_, ident, reads, pb):
            S.op("pe", lambda e: e.transpose(out=o, in_=in_, identity=ident), reads=reads, writes=[pb])

        def act(o, in_, func, reads, writes, **kw):
            S.op("act", lambda e: e.activation(out=o, in_=in_, func=func, **kw), reads=reads, writes=writes)

        def stt(o, in0, scalar, in1, op0, op1, reads, writes, eng="dve"):
            S.op(eng, lambda e: e.scalar_tensor_tensor(out=o, in0=in0, scalar=scalar, in1=in1, op0=op0, op1=op1),
                 reads=reads, writes=writes)

        def tt(o, in0, in1, op, reads, writes, eng="dve"):
            S.op(eng, lambda e: e.tensor_tensor(out=o, in0=in0, in1=in1, op=op), reads=reads, writes=writes)

        def ts(o, in0, s1, op0, reads, writes, s2=None, op1=None, eng="dve"):
            if op1 is None:
                S.op(eng, lambda e: e.tensor_scalar(out=o, in0=in0, scalar1=s1, scalar2=None, op0=op0),
                     reads=reads, writes=writes)
            else:
                S.op(eng, lambda e: e.tensor_scalar(out=o, in0=in0, scalar1=s1, scalar2=s2, op0=op0, op1=op1),
                     reads=reads, writes=writes)

        def cp(o, in_, reads, writes, eng="dve"):
            if eng == "act":
                S.op(eng, lambda e: e.activation(out=o, in_=in_, func=AF.Copy), reads=reads, writes=writes)
            else:
                S.op(eng, lambda e: e.tensor_copy(out=o, in_=in_), reads=reads, writes=writes)

        def recip(o, in_, reads, writes):
            S.op("dve", lambda e: e.reciprocal(out=o, in_=in_), reads=reads, writes=writes)

        def memset(ap, v, writes, eng="pool"):
            S.op(eng, lambda e: e.memset(ap, v), writes=writes)

        def asel(o, in_, pattern, cmp, fill, base, cm, reads, writes):
            S.op("pool", lambda e: e.affine_select(out=o, in_=in_, pattern=pattern, compare_op=cmp, fill=fill,
                                                   base=base, channel_multiplier=cm), reads=reads, writes=writes)

        memset(ident_f[:], 0.0, [Bc])
        asel(ident_f[:], ident_f[:], [[-1, 128]], ALU.not_equal, 1.0, 0, 1, [Bc], [Bc])
        cp(ident_b[:], ident_f[:], [Bc], [Bc], eng="pool")
        memset(ones_b[:], 1.0, [Bc])
        memset(ones_f[:], 1.0, [Bc])
        memset(maskneg[:], 0.0, [Bc])
        asel(maskneg[:], maskneg[:], [[1, 128]], ALU.is_ge, -30000.0, 0, -1, [Bc], [Bc])
        memset(hmask[:], 1.0, [Bc])
        asel(hmask[:], hmask[:], [[0, 8], [1, 64]], ALU.is_ge, 0.0, 0, -1, [Bc], [Bc])
        memset(rmask[:], 1.0, [Bc])
        memset(rmask[:].rearrange("p (c t) -> p c t", t=64)[:, :, 0:1], 0.0, [Bc])
        for l in range(2):
            memset(zhalo[l][:], 0.0, [zhb[l]])
        memset(S32[:], 0.0, S32b)
        memset(ccarry[:], 0.0, [ccb])

        for k in ["ab_in", "ab_out", "up0", "dn0", "c_in", "c_out", "up1", "dn1"]:
            rows = WSHAPES[k][0]
            step = 256
            for r0 in range(0, rows, step):
                S.dma("pool", wbf[k][r0:r0 + step, :], wf32[k][r0:r0 + step, :], writes=[wmatb[k]], sbuf=wmatb[k])

        def load_fm(src2d, nrows, dst_ap):
            S.dma("sp", stg[0:nrows, :], src2d, writes=[stgb], sbuf=stgb)
            ps, pb = psum("mm")
            tr(ps[:, 0:nrows], stg[0:nrows, :], ident_f[0:nrows, 0:nrows], [stgb, Bc], pb)
            cp(dst_ap, ps[:, 0:nrows], [pb], [parb])

        load_fm(mix_norm, 32, gmix[:, :])
        load_fm(ffn_norm, 32, gffn[:, :])
        for i in range(3):
            r0 = i * 128
            n = min(128, 264 - r0)
            load_fm(conv_w[r0:r0 + n, :], n, cw[:, r0:r0 + n])
        load_fm(conv_b[:, :], 88, cbias[:, :])
        load_fm(gamma, 12, gam[:, :])
        load_fm(vecs, 3, vec3[:, :])
        ogain = vec3[:, 0:1]
        qgain = vec3[:, 1:2]
        kgain = vec3[:, 2:3]
        act(gam[:, :], gam[:, :], AF.Exp, [parb], [parb])
        tt(lbt[:, :], gam[:, 0:4], gam[:, 4:8], ALU.add, [parb], [parb])
        tt(lbt[:, :], lbt[:, :], gam[:, 8:12], ALU.add, [parb], [parb])
        recip(lbt[:, :], lbt[:, :], [parb], [parb])
        tt(lbt[:, :], lbt[:, :], gam[:, 0:4], ALU.mult, [parb], [parb])
        ts(oml[:, :], lbt[:, :], -1.0, ALU.mult, [parb], [parb], s2=1.0, op1=ALU.add)
        for h in range(NH0):
            S.dma("sp", stg[:, :], sp_w[h], writes=[stgb], sbuf=stgb)
            ps, pb = psum("mm")
            tr(ps[:, 0:128], stg[:, :], ident_f[:, :], [stgb, Bc], pb)
            t_, tb_ = tmp()
            cp(t_[:, 0:128], ps[:, 0:128], [pb], [tb_])
            asel(t_[:, 0:128], t_[:, 0:128], [[1, 128]], ALU.is_ge, 0.0, 0, -1, [tb_], [tb_])
            cp(WcT[:, h, :], t_[:, 0:128], [tb_], [parb], eng="pool")
        S.dma("sp", vgain[:, :], v_norm.partition_broadcast(128), writes=[vgb], sbuf=vgb)
        S.dma("sp", spb[:, :], sp_b, writes=[spbb], sbuf=spbb)
        S.dma("sp", negbf[:, :], b_f, writes=[bfb], sbuf=bfb)
        ts(negbf[:, :], negbf[:, :], -1.0, ALU.mult, [bfb], [bfb])
        S.dma("sp", wf[:, :, :], wbf["c_in"][:, 4096:4096 + NHC].rearrange("(c p) n -> p c n", p=128),
              reads=[wmatb["c_in"]], writes=[wfb], sbuf=wfb, slow=True)

        gseq = weight_seq_group()
        wseq = gseq * NG
        wstate = {"pos": 0, "loaded": 0}

        def wload(i):
            name, r0, nk, c0, ncol = wseq[i]
            slot = i % NWB
            c0s = c0 if isinstance(c0, tuple) else (c0,)
            for ci, cc in enumerate(c0s):
                src = wbf[name][r0:r0 + nk * 128, cc:cc + ncol].rearrange("(c p) n -> p c n", p=128)
                S.dma("sp", wt[slot][:, 0:nk, ci * ncol:(ci + 1) * ncol], src, reads=[wmatb[name]],
                      writes=[wtb[slot]], sbuf=wtb[slot])

        def wacq(spec):
            i = wstate["pos"]
            assert wseq[i] == spec, (i, wseq[i], spec)
            lim = min(len(wseq), i + NWB)
            while wstate["loaded"] < lim:
                wload(wstate["loaded"])
                wstate["loaded"] += 1
            wstate["pos"] += 1
            s = i % NWB
            return wt[s], wtb[s]

        def load_group(g):
            t0 = g * GT
            for n in range(4):
                xi, xib = xin[n % 2], xinb[n % 2]
                S.dma("sp", xi[:, :], x[t0 + n * 128:t0 + (n + 1) * 128, :], writes=[xib], sbuf=xib)
                for q4 in range(4):
                    ps, pb = psum("mm")
                    for j in range(4):
                        c = 4 * q4 + j
                        tr(ps[:, j * 128:(j + 1) * 128], xi[:, c * 128:(c + 1) * 128], ident_f[:, :], [xib, Bc], pb)
                    cp(xT[:, 4 * q4:4 * q4 + 4, n * 128:(n + 1) * 128],
                       ps[:, :].rearrange("p (j t) -> p j t", j=4), [pb], xTb[4 * q4:4 * q4 + 4],
                       eng=("dve" if q4 % 2 == 0 else "act") if False else "dve")

        def store_group(g):
            t0 = g * GT
            for n in range(4):
                xi, xib = xin[n % 2], xinb[n % 2]
                for q4 in range(4):
                    ps, pb = psum("mm")
                    for j in range(4):
                        c = 4 * q4 + j
                        tr(ps[:, j * 128:(j + 1) * 128], xT[:, c, n * 128:(n + 1) * 128], ident_f[:, :], [xTb[c], Bc], pb)
                    S.op("act", lambda e, o=xi[:, q4 * 512:(q4 + 1) * 512], i_=ps[:, :]: e.activation(out=o, in_=i_, func=AF.Copy),
                         reads=[pb], writes=[xib])
                S.dma("sp", out[t0 + n * 128:t0 + (n + 1) * 128, :], xi[:, :], reads=[xib], writes=[], sbuf=xib)

        def norm(gt_, col0):
            ps, pb = psum("mm")
            for c in range(16):
                sq, sqb_ = sqbuf()
                act(sq[:, :], xT[:, c, :], AF.Square, [xTb[c]], [sqb_])
                mm(ps[:, :], ones_b[:, :], sq[:, :], c == 0, c == 15, [sqb_, Bc], pb)
            rs, rsb = tmp()
            act(rs[:, 0:512], ps[:, :], AF.Sqrt, [pb], [rsb], scale=1.0 / D, bias=EPS)
            recip(rs[:, 0:512], rs[:, 0:512], [rsb], [rsb])
            for c in range(16):
                stt(hT[:, c, :], xT[:, c, :], gt_[:, col0 + c:col0 + c + 1], rs[:, 0:512], ALU.mult, ALU.mult,
                    [xTb[c], rsb, parb], [hTb[c]])

        def proj_fm(w, wb_, nk, m, rhs_t, rhs_b, ps, pb, k0=0, first=True, last=True):
            for kc in range(nk):
                mm(ps[:, :], w[:, kc, m * 128:(m + 1) * 128], rhs_t[:, k0 + kc, :],
                   first and kc == 0, last and kc == nk - 1, [wb_, rhs_b[k0 + kc]], pb)

        def mixer_ab(g):
            norm(gmix, 0)
            for b in range(1):
                w, wb_ = wacq(("ab_in", 0, 16, AB_IN_COLS[0], 512))
                for m in range(4):
                    ps, pb = psum("mm")
                    proj_fm(w, wb_, 16, m, hT, hTb, ps, pb)
                    act(yT[:, 4 * b + m, :], ps[:, :], AF.Gelu, [pb], [yTb[4 * b + m]])
            for b in range(1):
                w, wb_ = wacq(("ab_in", 0, 16, AB_IN_COLS[1], 512))
                for n in range(4):
                    ps, pb = psum("mm")
                    for kc in range(16):
                        mm(ps[:, :], hT[:, kc, n * 128:(n + 1) * 128], w[:, kc, :], kc == 0, kc == 15, [wb_, hTb[kc]], pb)
                    vg, vgb_ = tmp()
                    act(vg[:, 0:512], ps[:, :], AF.Gelu, [pb], [vgb_])
                    s2, s2b = tmp()
                    tt(s2[:, 0:512], vg[:, 0:512], vg[:, 0:512], ALU.mult, [vgb_], [s2b])
                    r4, r4b = tmp()
                    S.op("dve", lambda e, o=r4[:, 0:4], i_=s2[:, 0:512].rearrange("p (h d) -> p h d", h=4):
                         e.tensor_reduce(out=o, in_=i_, axis=AX.X, op=ALU.add), reads=[s2b], writes=[r4b])
                    act(r4[:, 0:4], r4[:, 0:4], AF.Sqrt, [r4b], [r4b], scale=1.0 / 128, bias=EPS)
                    recip(r4[:, 0:4], r4[:, 0:4], [r4b], [r4b])
                    for hh in range(4):
                        h = 4 * b + hh
                        stt(vn[:, n, h * 128:(h + 1) * 128], vg[:, hh * 128:(hh + 1) * 128], r4[:, hh:hh + 1],
                            vgain[:, h * 128:(h + 1) * 128], ALU.mult, ALU.mult, [vgb_, r4b, vgb], [vnb[n]])
            for h in range(NH0):
                ps, pb = psum("mm")
                for n in range(4):
                    mm(ps[:, n * 128:(n + 1) * 128], vn[:, n, h * 128:(h + 1) * 128], WcT[:, h, :], True, False,
                       [vnb[n], parb], pb)
                    mm(ps[:, n * 128:(n + 1) * 128], ones_f[0:1, 0:128], spb[0:1, h * 128:(h + 1) * 128], False, True,
                       [Bc, spbb], pb)
                tt(yT[:, h, :], ps[:, :], yT[:, h, :], ALU.mult, [pb, yTb[h]], [yTb[h]])
            for b in range(1):
                w, wb_ = wacq(("ab_in", 0, 16, AB_IN_COLS[2], 512))
                for m in range(4):
                    h = 4 * b + m
                    ps, pb = psum("mm")
                    proj_fm(w, wb_, 16, m, hT, hTb, ps, pb)
                    t1, t1b = tmp()
                    act(t1[:, 0:512], ps[:, :], AF.Sigmoid, [pb], [t1b], scale=-1.0)
                    ts(t1[:, 0:512], t1[:, 0:512], oml[:, h:h + 1], ALU.mult, [t1b, parb], [t1b])
                    t2, t2b = tmp()
                    act(t2[:, 0:512], t1[:, 0:512], AF.Ln, [t1b], [t2b], scale=-1.0, bias=1.0)
                    G, Gb = tmp()
                    S.op("dve", lambda e, o=G[:, 0:512], d0=rmask[:, :], d1=t2[:, 0:512]:
                         e.tensor_tensor_scan(out=o, data0=d0, data1=d1, initial=0.0, op0=ALU.mult, op1=ALU.add),
                         reads=[t2b, Bc], writes=[Gb])
                    t3, t3b = tmp()
                    act(t3[:, 0:512], G[:, 0:512], AF.Exp, [Gb], [t3b], scale=-1.0)
                    tt(kT[:, h, :], t1[:, 0:512], t3[:, 0:512], ALU.mult, [t1b, t3b], [kTb[h]])
                    act(eG[:, m, :], G[:, 0:512], AF.Exp, [Gb], [eGb[m]])
                    cp(eGend[:, h * 8:(h + 1) * 8].rearrange("p (c o) -> p c o", o=1),
                       eG[:, m, :].rearrange("p (c t) -> p c t", t=64)[:, :, 63:64], [eGb[m]], [eGendb], eng="act")
                w, wb_ = wacq(("ab_in", 0, 16, AB_IN_COLS[3], 512))
                for m in range(4):
                    h = 4 * b + m
                    ps, pb = psum("mm")
                    proj_fm(w, wb_, 16, m, hT, hTb, ps, pb)
                    tt(yT[:, NH0 + h, :], ps[:, :], eG[:, m, :], ALU.mult, [pb, eGb[m]], [yTb[NH0 + h]])
            for b in range(1):
                w, wb_ = wacq(("ab_in", 0, 16, AB_IN_COLS[4], 512))
                for cch in range(8):
                    ps, pb = psum("mm")
                    for kc in range(16):
                        mm(ps[0:64, :], hT[:, kc, cch * 64:(cch + 1) * 64], w[:, kc, :], kc == 0, kc == 15,
                           [wb_, hTb[kc]], pb)
                    act(it[0:64, cch, b * 512:(b + 1) * 512], ps[0:64, :], AF.Copy, [pb], [itb])

            def pre(h):
                sl = h % 2
                pst, pstb = psum("mm")
                pbt = pst[:, :].bitcast(BF16)
                for cch in range(8):
                    tr(pbt[0:64, cch * 128:(cch + 1) * 128], kT[:, h, cch * 64:(cch + 1) * 64], ident_b[:, :],
                       [kTb[h], Bc], pstb)
                act(ktok[sl][0:64, :, :], pbt[0:64, 0:1024].rearrange("p (c k) -> p c k", c=8), AF.Copy, [pstb], [ktokb[sl]])
                pkv = [psum("aux"), psum("aux")]
                for cch in range(8):
                    pk, pkb = pkv[cch // 4]
                    mm(pk[:, (cch % 4) * 128:(cch % 4 + 1) * 128], ktok[sl][0:64, cch, :], it[0:64, cch, h * 128:(h + 1) * 128],
                       True, True, [ktokb[sl], itb], pkb)
                act(Sst[sl][:, 0, :], S32[:, h, :], AF.Copy, [S32b[h]], [Sstb[sl]])
                for cch in range(8):
                    pk, pkb = pkv[cch // 4]
                    tt(S32[:, h, :], S32[:, h, :], pk[:, (cch % 4) * 128:(cch % 4 + 1) * 128], ALU.add, [S32b[h], pkb], [S32b[h]])
                    ts(S32[:, h, :], S32[:, h, :], eGend[:, h * 8 + cch:h * 8 + cch + 1], ALU.mult, [S32b[h], eGendb], [S32b[h]])
                    if cch < 7:
                        act(Sst[sl][:, cch + 1, :], S32[:, h, :], AF.Copy, [S32b[h]], [Sstb[sl]])
                psc, pscb = psum("mm")
                for cch in range(8):
                    mm(psc[0:64, cch * 64:(cch + 1) * 64], kT[:, h, cch * 64:(cch + 1) * 64], yT[:, NH0 + h, cch * 64:(cch + 1) * 64],
                       True, True, [kTb[h], yTb[NH0 + h]], pscb)
                tt(sTt[sl][0:64, :], psc[0:64, :], hmask[0:64, :, :].rearrange("p c t -> p (c t)"), ALU.mult,
                   [pscb, Bc], [sTb[sl]])

            def post(h):
                sl = h % 2
                po, pob = psum("aux")
                for cch in range(8):
                    cs = slice(cch * 64, (cch + 1) * 64)
                    mm(po[:, cs], it[0:64, cch, h * 128:(h + 1) * 128], sTt[sl][0:64, cs], True, False, [itb, sTb[sl]], pob)
                    mm(po[:, cs], Sst[sl][:, cch, :], yT[:, NH0 + h, cs], False, True, [Sstb[sl], yTb[NH0 + h]], pob)
                sq, sqb_ = sqbuf()
                act(sq[:, :], po[:, :], AF.Square, [pob], [sqb_])
                pss, pssb = psum("mm")
                mm(pss[:, :], ones_b[:, :], sq[:, :], True, True, [sqb_, Bc], pssb)
                rs, rsb = tmp()
                act(rs[:, 0:512], pss[:, :], AF.Sqrt, [pssb], [rsb], scale=1.0 / 128, bias=EPS)
                recip(rs[:, 0:512], rs[:, 0:512], [rsb], [rsb])
                stt(yT[:, NH0 + h, :], po[:, :], ogain, rs[:, 0:512], ALU.mult, ALU.mult, [pob, rsb, parb], [yTb[NH0 + h]])

            pre(0)
            for h in range(NH0):
                if h < NH0 - 1:
                    pre(h + 1)
                post(h)
            for b in range(1):
                w, wb_ = wacq(("ab_in", 0, 16, AB_IN_COLS[5], 512))
                for m in range(4):
                    h = 4 * b + m
                    ps, pb = psum("mm")
                    proj_fm(w, wb_, 16, m, hT, hTb, ps, pb)
                    t1, t1b = tmp()
                    act(t1[:, 0:512], ps[:, :], AF.Silu, [pb], [t1b])
                    tt(yT[:, NH0 + h, :], t1[:, 0:512], yT[:, NH0 + h, :], ALU.mult, [t1b, yTb[NH0 + h]], [yTb[NH0 + h]])
            if dbg == "y0" and g == 0:
                S.dma("sp", dbg_y, yT[:, :, :], reads=yTb, writes=[], sbuf=yTb[0])
            out_proj("ab_out", yT, yTb)

        xcnt = {"n": 0}

        def allreduce_add(parts, cb):
            n = xcnt["n"]
            xcnt["n"] += 1
            k = n % NCC
            xc, xcb_ = xch[n % 2], xchb[n % 2]
            for m in range(4):
                ps, pb = parts[m]
                act(xc[:, m, :], ps[:, :], AF.Copy, [pb], [xcb_])
            S.dma("sp", cc_in[k].ap().rearrange("p (m t) -> p m t", m=4), xc[:, :, :], reads=[xcb_], writes=[ccib[k]], sbuf=xcb_)
            S.async_op("pool", lambda e, k=k: e.collective_compute("AllReduce", ALU.add, replica_groups=PAIRS,
                                                                 ins=[cc_in[k].ap().opt()], outs=[cc_out[k].ap().opt()]),
                       reads=[ccib[k]], writes=[ccob[k]], sbuf=ccob[k], inc=1)
            flush_acc()
            xcnt["pend"] = (k, cb)

        def flush_acc():
            if xcnt.get("pend") is None:
                return
            k, cb = xcnt["pend"]
            xcnt["pend"] = None
            S.async_op("pool", lambda e, k=k, cb=cb: e.dma_start(out=xT[:, 4 * cb:4 * cb + 4, :],
                                                               in_=cc_out[k].ap().rearrange("p (m t) -> p m t", m=4),
                                                               accum_op=ALU.add),
                       reads=[ccob[k]] + xTb[4 * cb:4 * cb + 4], writes=xTb[4 * cb:4 * cb + 4], sbuf=accb[k])

        def out_proj(name, src, srcb):
            for cb in range(4):
                w, wb_ = wacq((name, 0, YC, cb * 512, 512))
                parts = []
                for m in range(4):
                    ps, pb = psum("aux" if cb % 2 == 0 else "mm")
                    proj_fm(w, wb_, YC, m, src, srcb, ps, pb)
                    parts.append((ps, pb))
                allreduce_add(parts, cb)
            flush_acc()

        def ffn(l):
            norm(gffn, 16 * l)
            up = "up%d" % l

            def conv(ps, pb, jj):
                acc, accb = tmp()
                zc, zcb = tmp()

                wcol = lambda j: cw[:, (l * 3 + j) * 2 * NFC + jj:(l * 3 + j) * 2 * NFC + jj + 1]
                act(acc[:, 0:512], ps[:, :], AF.Identity, [pb, parb], [accb], scale=wcol(2),
                    bias=cbias[:, l * 2 * NFC + jj:l * 2 * NFC + jj + 1])
                act(zc[:, 2:514], ps[:, :], AF.Copy, [pb], [zcb])
                cp(zc[:, 0:2], zhalo[l][:, jj, :], [zhb[l]], [zcb], eng="act")
                cp(zhalo[l][:, jj, :], zc[:, 512:514], [zcb], [zhb[l]], eng="act")
                stt(acc[:, 0:512], zc[:, 1:513], wcol(1), acc[:, 0:512], ALU.mult, ALU.add, [zcb, accb, parb], [accb])
                stt(acc[:, 0:512], zc[:, 0:512], wcol(0), acc[:, 0:512], ALU.mult, ALU.add, [zcb, accb, parb], [accb])
                return acc, accb

            for bi in range(NFC // 2):
                wa, wab = wacq((up, 0, 16, (bi * 256, FH + bi * 256), 256))
                for m in range(2):
                    j = 2 * bi + m
                    psa, pba = psum("mm")
                    proj_fm(wa, wab, 16, m, hT, hTb, psa, pba)
                    psb_, pbb = psum("mm")
                    proj_fm(wa, wab, 16, 2 + m, hT, hTb, psb_, pbb)
                    aa, aab = conv(psa, pba, j)
                    bb, bbb = conv(psb_, pbb, NFC + j)
                    act(aa[:, 0:512], aa[:, 0:512], AF.Silu, [aab], [aab])
                    tt(actT[:, j, :], aa[:, 0:512], bb[:, 0:512], ALU.mult, [aab, bbb], [actb[j]])
            dn = "dn%d" % l
            for cb in range(4):
                accs = [psum("aux" if cb % 2 == 0 else "mm") for _ in range(4)]
                for kb in range(2):
                    nk = 16 if kb < 1 else NFC - 16
                    w, wb_ = wacq((dn, kb * 2048, nk, cb * 512, 512))
                    for m in range(4):
                        ps, pb = accs[m]
                        proj_fm(w, wb_, nk, m, actT, actb, ps, pb, k0=kb * 16, first=(kb == 0), last=(kb == 1))
                allreduce_add(accs, cb)
            flush_acc()

        def qknorm(ps, pb, gain_ap, o, writes):
            sq, sqb_ = sqbuf()
            act(sq[:, :], ps[:, :], AF.Square, [pb], [sqb_])
            pss, pssb = psum("aux")
            mm(pss[:, :], ones_b[:, :], sq[:, :], True, True, [sqb_, Bc], pssb)
            rs, rsb = tmp()
            act(rs[:, 0:512], pss[:, :], AF.Sqrt, [pssb], [rsb], scale=1.0 / 128, bias=EPS)
            recip(rs[:, 0:512], rs[:, 0:512], [rsb], [rsb])
            stt(o, ps[:, :], gain_ap, rs[:, 0:512], ALU.mult, ALU.mult, [pb, rsb, parb], writes)

        def mixer_c(g):
            t0 = g * GT
            norm(gmix, 16)
            for b in range(2):
                w, wb_ = wacq(("c_in", 0, 16, b * 512, 512))
                for m in range(4):
                    h = 4 * b + m
                    ps, pb = psum("mm")
                    proj_fm(w, wb_, 16, m, hT, hTb, ps, pb)
                    qknorm(ps, pb, qgain, yT[:, h, :], [yTb[h]])
            for b in range(2):
                w, wb_ = wacq(("c_in", 0, 16, 1024 + b * 512, 512))
                for m in range(4):
                    h = 4 * b + m
                    ps, pb = psum("mm")
                    proj_fm(w, wb_, 16, m, hT, hTb, ps, pb)
                    sl = h % 2
                    qknorm(ps, pb, kgain, kst[sl][:, :], [kstb[sl]])
                    S.dma("sp", kT_d[h, :, t0:t0 + GT], kst[sl][:, :], reads=[kstb[sl]], writes=[kdb[h]], sbuf=kstb[sl])
            for b in range(2):
                w, wb_ = wacq(("c_in", 0, 16, 2048 + b * 512, 512))
                sl = b % 2
                for n in range(4):
                    ps, pb = psum("mm")
                    for kc in range(16):
                        mm(ps[:, :], hT[:, kc, n * 128:(n + 1) * 128], w[:, kc, :], kc == 0, kc == 15, [wb_, hTb[kc]], pb)
                    act(vst[sl][:, n, :], ps[:, :], AF.Copy, [pb], [vstb[sl]])
                for hh in range(4):
                    h = 4 * b + hh
                    S.dma("sp", v_d[h, :, 4 * g:4 * g + 4, :], vst[sl][:, :, hh * 128:(hh + 1) * 128],
                          reads=[vstb[sl]], writes=[vdb[h]], sbuf=vstb[sl])
            psf, psfb = psum("mm")
            for kc in range(16):
                mm(psf[0:NHC, :], wf[:, kc, :], hT[:, kc, :], kc == 0, kc == 15, [wfb, hTb[kc]], psfb)
            act(e16[:, :], psf[0:NHC, :], AF.Exp, [psfb, bfb], [e16b], scale=-1.0, bias=negbf[:, 0:1])
            act(e16[:, :], e16[:, :], AF.Ln, [e16b], [e16b], bias=1.0)
            S.op("dve", lambda e: e.tensor_tensor_scan(out=c16[:, :], data0=ones_f[0:NHC, :], data1=e16[:, :],
                                                       initial=ccarry[:, 0:1], op0=ALU.mult, op1=ALU.subtract),
                 reads=[e16b, Bc, ccb], writes=[c16b])
            cp(ccarry[:, 0:1], c16[:, 511:512], [c16b], [ccb])
            ptr, ptrb = psum("mm")
            for n in range(4):
                tr(ptr[:, n * NHC:(n + 1) * NHC], c16[0:NHC, n * 128:(n + 1) * 128], ident_f[0:NHC, 0:NHC], [c16b, Bc], ptrb)
            ts(negcT[:, 4 * g * NHC:(4 * g + 4) * NHC], ptr[:, 0:4 * NHC], -1.0, ALU.mult, [ptrb], [negcb])

            nJ = 4 * g + 4
            npc = (nJ + 7) // 8
            pieces = [(h, pc) for h in range(NHC) for pc in range(npc)]
            kvs = {"loaded": 0}

            def kvload(i):
                h, pc = pieces[i]
                sl = i % NKV
                nb = min(8, nJ - pc * 8)
                S.dma("sp", kpc[sl][:, 0:nb * 128], kT_d[h, :, pc * 1024:pc * 1024 + nb * 128],
                      reads=[kdb[h]], writes=[kpcb[sl]], sbuf=kpcb[sl])
                S.dma("sp", vpc[sl][:, 0:nb, :], v_d[h, :, pc * 8:pc * 8 + nb, :],
                      reads=[vdb[h]], writes=[vpcb[sl]], sbuf=vpcb[sl])

            def kvacq(i):
                lim = min(len(pieces), i + NKV - 1)
                while kvs["loaded"] < lim:
                    kvload(kvs["loaded"])
                    kvs["loaded"] += 1
                return i % NKV

            scale = 128.0 ** -0.5
            pi = 0
            pcnt = 0
            for h in range(NHC):
                ts(e16[:, :], c16[:, :], ident_f[0:NHC, h:h + 1], ALU.mult, [c16b, Bc], [e16b])
                pcb_, pcbb = psum("mm")
                mm(pcb_[:, :], ones_f[0:NHC, 0:128], e16[:, :], True, True, [Bc, e16b], pcbb)
                cs_ = h % 2
                act(cbt[cs_][:, :], pcb_[:, :], AF.Copy, [pcbb], [cbtb[cs_]])
                O, Ob = psum("aux")
                L, Lb = psum("aux")
                slots = {}

                def qk(J):
                    pc = J // 8
                    if pc not in slots:
                        slots[pc] = kvacq(pi + pc)
                    sl = slots[pc]
                    lo = max(0, J - 4 * g) * 128
                    psc, pscb = psum("mm")
                    mm(psc[:, lo:512], kpc[sl][:, (J % 8) * 128:(J % 8 + 1) * 128], yT[:, h, lo:512], True, True,
                       [kpcb[sl], yTb[h]], pscb)
                    return psc, pscb, lo, sl

                qq = [qk(0)]
                if nJ > 1:
                    qq.append(qk(1))
                for J in range(nJ):
                    psc, pscb, lo, sl = qq[J]
                    tm, tmb = tmp()
                    stt(tm[:, lo:512], psc[:, lo:512], scale, cbt[cs_][:, lo:512], ALU.mult, ALU.add,
                        [pscb, cbtb[cs_]], [tmb])
                    if J >= 4 * g:
                        tt(tm[:, lo:lo + 128], tm[:, lo:lo + 128], maskneg[:, :], ALU.add, [tmb, Bc], [tmb])
                    pt, ptb = pTt[pcnt % 3], pTb[pcnt % 3]
                    pcnt += 1
                    act(pt[:, lo:512], tm[:, lo:512], AF.Exp, [tmb, negcb], [ptb],
                        bias=negcT[:, J * NHC + h:J * NHC + h + 1])
                    if J + 2 < nJ:
                        qq.append(qk(J + 2))
                    mm(O[:, lo:512], vpc[sl][:, J % 8, :], pt[:, lo:512], J == 0, J == nJ - 1, [vpcb[sl], ptb], Ob)
                    mm(L[:, lo:512], ones_b[:, :], pt[:, lo:512], J == 0, J == nJ - 1, [Bc, ptb], Lb)
                pi += npc
                rl, rlb = tmp()
                recip(rl[:, 0:512], L[:, :], [Lb], [rlb])
                tt(yT[:, h, :], O[:, :], rl[:, 0:512], ALU.mult, [Ob, rlb], [yTb[h]])
            for b in range(2):
                w, wb_ = wacq(("c_in", 0, 16, 3072 + b * 512, 512))
                for m in range(4):
                    h = 4 * b + m
                    ps, pb = psum("mm")
                    proj_fm(w, wb_, 16, m, hT, hTb, ps, pb)
                    t1, t1b = tmp()
                    act(t1[:, 0:512], ps[:, :], AF.Sigmoid, [pb], [t1b])
                    tt(yT[:, h, :], t1[:, 0:512], yT[:, h, :], ALU.mult, [t1b, yTb[h]], [yTb[h]])
            out_proj("c_out", yT, yTb)

        L0set = vnb + kTb + eGb + [itb] + ktokb
        FFNset = actb
        L1set = vstb + kpcb + vpcb + cbtb + kstb + pTb + [c16b, e16b]
        for g in range(NG):
            load_group(g)
            S.alias(FFNset, L0set)
            mixer_ab(g)
            if stop == "mix0":
                store_group(g)
                S.alias(L0set, FFNset)
                wstate["pos"] += len(gseq) - 10
                wstate["loaded"] = max(wstate["loaded"], wstate["pos"])
                continue
            S.alias(L0set, FFNset)
            ffn(0)
            if stop == "ffn0":
                store_group(g)
                wstate["pos"] += len(gseq) - 10 - 19
                wstate["loaded"] = max(wstate["loaded"], wstate["pos"])
                continue
            S.alias(FFNset, L1set)
            mixer_c(g)
            if stop == "mix1":
                store_group(g)
                S.alias(L1set, FFNset)
                wstate["pos"] += 19
                wstate["loaded"] = max(wstate["loaded"], wstate["pos"])
                continue
            S.alias(L1set, FFNset)
            ffn(1)
            store_group(g)
        S.emit(final_waits=xinb[0:1])
    return nc


def make_in_map(inp, b, j, NG=8):
    T = NG * GT
    A = lambda k: np.asarray(inp[k], dtype=np.float32)
    f = lambda a: np.ascontiguousarray(a, dtype=np.float32)
    hs = slice(j * 512, (j + 1) * 512)
    w_in = A("ab_w_in")[0]
    ab_in = np.concatenate([w_in[:, 0 + j * 512:0 + (j + 1) * 512],
                            w_in[:, 1024 + j * 512:1024 + (j + 1) * 512],
                            w_in[:, 3072 + j * 512:3072 + (j + 1) * 512],
                            w_in[:, 2048 + j * 512:2048 + (j + 1) * 512],
                            w_in[:, 4096 + j * 512:4096 + (j + 1) * 512],
                            w_in[:, 5120 + j * 512:5120 + (j + 1) * 512]], axis=1)
    w_out = A("ab_w_out")[0]
    ab_out = np.concatenate([w_out[hs], w_out[1024 + j * 512:1024 + (j + 1) * 512]], axis=0)
    fs = slice(j * FH, (j + 1) * FH)
    fs2 = slice(5632 + j * FH, 5632 + (j + 1) * FH)
    up = A("ffn_w_up")
    cwv = A("ffn_conv_w")
    cbv = A("ffn_conv_b")
    c_in = A("c_w_in")[0]
    cs = slice(j * 1024, (j + 1) * 1024)
    c_in_h = np.concatenate([c_in[:, 0 + j * 1024:0 + (j + 1) * 1024], c_in[:, 2048 + j * 1024:2048 + (j + 1) * 1024],
                             c_in[:, 4096 + j * 1024:4096 + (j + 1) * 1024], c_in[:, 6144 + j * 1024:6144 + (j + 1) * 1024],
                             c_in[:, 8192 + j * 8:8192 + (j + 1) * 8]], axis=1)
    m = {
        "x": f(A("x")[b, :T]),
        "mix_norm": f(A("mix_norm").reshape(32, 128)),
        "ffn_norm": f(A("ffn_norm").reshape(32, 128)),
        "ffn_conv_w": f(np.concatenate([cwv[:, :, fs], cwv[:, :, fs2]], axis=2).reshape(264, 128)),
        "ffn_conv_b": f(np.concatenate([cbv[:, fs], cbv[:, fs2]], axis=1).reshape(88, 128)),
        "hgrn_gamma": f(A("hgrn_gamma")[:, hs].reshape(12, 128)),
        "vecs": f(np.stack([A("hgrn_o_norm")[0], A("c_q_norm")[0], A("c_k_norm")[0]], axis=0)),
        "ab_v_norm": f(A("ab_v_norm")[0, hs]),
        "ab_sp_w": f(A("ab_sp_w")[0, 4 * j:4 * j + 4]),
        "ab_sp_b": f(A("ab_sp_b")[0, 4 * j:4 * j + 4].reshape(1, 512)),
        "c_b_f": f(A("c_b_f")[0, 8 * j:8 * j + 8].reshape(8, 1)),
        "ab_w_in": f(ab_in),
        "ab_w_out": f(ab_out),
        "ffn_w_up0": f(np.concatenate([up[0][:, fs], up[0][:, fs2]], axis=1)),
        "ffn_w_up1": f(np.concatenate([up[1][:, fs], up[1][:, fs2]], axis=1)),
        "ffn_w_down0": f(A("ffn_w_down")[0][fs]),
        "ffn_w_down1": f(A("ffn_w_down")[1][fs]),
        "c_w_in": f(c_in_h),
        "c_w_out": f(A("c_w_out")[0][cs]),
    }
    return m


def kernel(**inputs):
    nc = build(8)
    B = inputs["x"].shape[0]
    in_maps = [make_in_map(inputs, c // 2, c % 2) for c in range(2 * B)]
    res = run_bass_kernel_spmd(nc, in_maps, core_ids=list(range(2 * B)))
    return np.stack([np.asarray(res.results[2 * b]["out"], dtype=np.float32) for b in range(B)], axis=0)
```
